# Optimizing a Trainium2 kernel written in Bass

```python
import math
import jax, jax.numpy as jnp
from jax import lax
import numpy as np

D_MODEL = 1024
BATCH = 32
SEQ = 2048
DEPTH = 1
DEC_BATCH = 1
DEC_SEQ = 16384
PAST_LEN = 128

MIX_WIDTH = D_MODEL
HG_WIDTH = MIX_WIDTH // 2
S5_WIDTH = MIX_WIDTH - HG_WIDTH
HG_HEADS = 4
HG_DK = 128
HG_DV = HG_WIDTH // HG_HEADS
HG_FDIM = HG_HEADS * HG_DK
HG_CHUNK = 64
S5_GROUP = 16
S5_GROUPS = S5_WIDTH // S5_GROUP
S5_STATE = 64
D_FF = 4 * D_MODEL
EPS = 1e-6
DT_MIN = 1e-3
DT_MAX = 1e-1
IN_WIDTH = 3 * HG_FDIM + 2 * HG_WIDTH + S5_WIDTH
IN_OFFSETS = (HG_FDIM, 2 * HG_FDIM, 3 * HG_FDIM, 3 * HG_FDIM + HG_WIDTH, 3 * HG_FDIM + 2 * HG_WIDTH)

kernel_name = "hymba_hgrn2_s5_bidir_encoder"


def rmsnorm(x, g):
    xf = x.astype(jnp.float32)
    y = xf * lax.rsqrt(jnp.mean(xf * xf, axis=-1, keepdims=True) + EPS) * g.astype(jnp.float32)
    return y.astype(x.dtype)


def _flip(t):
    return jnp.flip(t, axis=1)


def _chunk(t):
    b, s, h, d = t.shape
    return t.reshape(b, s // HG_CHUNK, HG_CHUNK, h, d).transpose(1, 0, 3, 2, 4)


def hgrn2_direction(q, k, lf, v):
    b, s, h, dk = q.shape
    dv = v.shape[-1]
    mask = jnp.tril(jnp.ones((HG_CHUNK, HG_CHUNK), dtype=bool))[:, :, None]

    def step(S, inp):
        qc, kc, lfc, vc = inp
        bc = jnp.cumsum(lfc, axis=-2)
        inter = jnp.einsum('bhtd,bhde->bhte', qc * jnp.exp(bc), S)
        diff = bc[:, :, :, None, :] - bc[:, :, None, :, :]
        decay = jnp.exp(jnp.where(mask, diff, -jnp.inf))
        att = jnp.einsum('bhtd,bhsd,bhtsd->bhts', qc, kc, decay)
        intra = jnp.einsum('bhts,bhse->bhte', att, vc)
        btot = bc[:, :, -1:, :]
        S_new = jnp.exp(btot[:, :, 0, :])[..., None] * S + jnp.einsum(
            'bhsd,bhse->bhde', kc * jnp.exp(btot - bc), vc)
        return S_new, inter + intra

    S0 = jnp.zeros((b, h, dk, dv), jnp.float32)
    _, o = lax.scan(step, S0, (_chunk(q), _chunk(k), _chunk(lf), _chunk(v)))
    return o.transpose(1, 0, 3, 2, 4).reshape(b, s, h, dv)


def hgrn2_mixer(q, zf_f, zf_b, vi, g, lb, onorm_g):
    b, s, _ = q.shape

    def heads(t, d):
        return t.astype(jnp.float32).reshape(b, s, HG_HEADS, d)

    qh = heads(q, HG_DK)
    vh = heads(vi, HG_DV)

    def gates(z, lbd):
        z = heads(z, HG_DK)
        lbd = lbd.reshape(HG_HEADS, HG_DK)
        f = lbd + (1.0 - lbd) * jax.nn.sigmoid(z)
        return jnp.log(f), (1.0 - lbd) * jax.nn.sigmoid(-z)

    lf_f, k_f = gates(zf_f, lb[0])
    lf_b, k_b = gates(zf_b, lb[1])
    o_f = hgrn2_direction(qh, k_f, lf_f, vh)
    o_b = _flip(hgrn2_direction(_flip(qh), _flip(k_b), _flip(lf_b), _flip(vh)))
    o = o_f + o_b
    o = o * lax.rsqrt(jnp.mean(o * o, axis=-1, keepdims=True) + EPS)
    return o.reshape(b, s, HG_WIDTH) * onorm_g.astype(jnp.float32) * jax.nn.silu(g.astype(jnp.float32))


def s5_direction(u, lam_re, lam_im, log_dt, b_re, b_im, c_re, c_im):
    s = u.shape[1]
    dt = jnp.exp(log_dt)[:, None]
    mag = jnp.exp(lam_re * dt)
    ai = lam_im * dt
    abar_re = mag * jnp.cos(ai)
    abar_im = mag * jnp.sin(ai)
    den = lam_re * lam_re + lam_im * lam_im
    nr = abar_re - 1.0
    ni = abar_im
    cr = ((nr * lam_re + ni * lam_im) / den)[..., None]
    ci = ((ni * lam_re - nr * lam_im) / den)[..., None]
    bb_re = cr * b_re - ci * b_im
    bb_im = cr * b_im + ci * b_re
    bu_re = jnp.einsum('bsgc,gpc->bsgp', u, bb_re)
    bu_im = jnp.einsum('bsgc,gpc->bsgp', u, bb_im)
    a_re = jnp.broadcast_to(abar_re, (1, s) + abar_re.shape)
    a_im = jnp.broadcast_to(abar_im, (1, s) + abar_im.shape)

    def combine(e1, e2):
        a1r, a1i, b1r, b1i = e1
        a2r, a2i, b2r, b2i = e2
        return (a1r * a2r - a1i * a2i,
                a1r * a2i + a1i * a2r,
                a2r * b1r - a2i * b1i + b2r,
                a2r * b1i + a2i * b1r + b2i)

    _, _, xr, xi = lax.associative_scan(combine, (a_re, a_im, bu_re, bu_im), axis=1)
    return jnp.einsum('bsgp,gcp->bsgc', xr, c_re) - jnp.einsum('bsgp,gcp->bsgc', xi, c_im)


def s5_mixer(u, lam_re, lam_im, log_dt, b_re, b_im, c_re, c_im, d, glu_w, glu_b, merge_g):
    b, s, _ = u.shape
    f32 = jnp.float32
    uf = u.astype(f32)
    ug = uf.reshape(b, s, S5_GROUPS, S5_GROUP)
    lam_re = lam_re.astype(f32); lam_im = lam_im.astype(f32); log_dt = log_dt.astype(f32)
    b_re = b_re.astype(f32); b_im = b_im.astype(f32); c_re = c_re.astype(f32); c_im = c_im.astype(f32)
    y_f = s5_direction(ug, lam_re[0], lam_im[0], log_dt[0], b_re[0], b_im[0], c_re[0], c_im[0])
    y_b = _flip(s5_direction(_flip(ug), lam_re[1], lam_im[1], log_dt[1], b_re[1], b_im[1], c_re[1], c_im[1]))
    y = (y_f + y_b).reshape(b, s, S5_WIDTH) + d.astype(f32) * uf
    z = jax.nn.gelu(y)
    z = z * jax.nn.sigmoid(z @ glu_w.astype(f32) + glu_b.astype(f32))
    z = z * lax.rsqrt(jnp.mean(z * z, axis=-1, keepdims=True) + EPS)
    return z * merge_g.astype(f32)


def trunk(x, norm1_g, w_in, hgrn_lb, hgrn_onorm_g, s5_lambda_re, s5_lambda_im, s5_log_dt,
          s5_b_re, s5_b_im, s5_c_re, s5_c_im, s5_d, s5_glu_w, s5_glu_b, s5_merge_g,
          w_out, norm2_g, w_ff1, w_ff2, norm_f_g):
    lbs = jnp.cumsum(jax.nn.softmax(hgrn_lb.astype(jnp.float32), axis=1), axis=1)
    for l in range(DEPTH):
        h = rmsnorm(x, norm1_g[l])
        proj = h @ w_in[l]
        q, zf_f, zf_b, vi, g, u = jnp.split(proj, IN_OFFSETS, axis=-1)
        ya = hgrn2_mixer(q, zf_f, zf_b, vi, g, lbs[:, l], hgrn_onorm_g[l])
        yb = s5_mixer(u, s5_lambda_re[l], s5_lambda_im[l], s5_log_dt[l], s5_b_re[l], s5_b_im[l],
                      s5_c_re[l], s5_c_im[l], s5_d[l], s5_glu_w[l], s5_glu_b[l], s5_merge_g[l])
        y = jnp.concatenate([ya, yb], axis=-1).astype(x.dtype) @ w_out[l]
        x = x + y
        h = rmsnorm(x, norm2_g[l])
        x = x + jnp.square(jax.nn.relu(h @ w_ff1[l])) @ w_ff2[l]
    return rmsnorm(x, norm_f_g)


def setup_inputs(seed: int = 0) -> dict:
    key = jax.random.key(seed)
    ks = jax.random.split(key, 24)
    n = jax.random.normal
    f32 = jnp.float32
    L, G, P, C = DEPTH, S5_GROUPS, S5_STATE, S5_GROUP
    return {
        "x_prompt": n(ks[0], (BATCH, SEQ, D_MODEL), f32),
        "x_sample": n(ks[1], (DEC_BATCH, DEC_SEQ, D_MODEL), f32),
        "norm1_g": 1.0 + 0.02 * n(ks[2], (L, D_MODEL), f32),
        "w_in": n(ks[3], (L, D_MODEL, IN_WIDTH), f32) * D_MODEL ** -0.5,
        "hgrn_lb": 0.5 * n(ks[4], (2, L + 1, HG_FDIM), f32),
        "hgrn_onorm_g": 1.0 + 0.02 * n(ks[5], (L, HG_WIDTH), f32),
        "s5_lambda_re": -0.5 + 0.01 * n(ks[6], (L, 2, G, P), f32),
        "s5_lambda_im": jnp.pi * jnp.arange(P, dtype=f32) + 0.01 * n(ks[7], (L, 2, G, P), f32),
        "s5_log_dt": jax.random.uniform(ks[8], (L, 2, G), f32, math.log(DT_MIN), math.log(DT_MAX)),
        "s5_b_re": n(ks[9], (L, 2, G, P, C), f32) * (1.0 / (2 * C)) ** 0.5,
        "s5_b_im": n(ks[10], (L, 2, G, P, C), f32) * (1.0 / (2 * C)) ** 0.5,
        "s5_c_re": n(ks[11], (L, 2, G, C, P), f32) * (1.0 / (2 * P)) ** 0.5,
        "s5_c_im": n(ks[12], (L, 2, G, C, P), f32) * (1.0 / (2 * P)) ** 0.5,
        "s5_d": n(ks[13], (L, S5_WIDTH), f32),
        "s5_glu_w": n(ks[14], (L, S5_WIDTH, S5_WIDTH), f32) * S5_WIDTH ** -0.5,
        "s5_glu_b": 0.01 * n(ks[15], (L, S5_WIDTH), f32),
        "s5_merge_g": 1.0 + 0.02 * n(ks[16], (L, S5_WIDTH), f32),
        "w_out": n(ks[17], (L, MIX_WIDTH, D_MODEL), f32) * MIX_WIDTH ** -0.5,
        "norm2_g": 1.0 + 0.02 * n(ks[18], (L, D_MODEL), f32),
        "w_ff1": n(ks[19], (L, D_MODEL, D_FF), f32) * D_MODEL ** -0.5,
        "w_ff2": n(ks[20], (L, D_FF, D_MODEL), f32) * D_FF ** -0.5,
        "norm_f_g": 1.0 + 0.02 * n(ks[21], (D_MODEL,), f32),
    }


def reference(x_prompt, x_sample, norm1_g, w_in, hgrn_lb, hgrn_onorm_g, s5_lambda_re, s5_lambda_im,
              s5_log_dt, s5_b_re, s5_b_im, s5_c_re, s5_c_im, s5_d, s5_glu_w, s5_glu_b, s5_merge_g,
              w_out, norm2_g, w_ff1, w_ff2, norm_f_g):
    y_prompt = trunk(x_prompt, norm1_g, w_in, hgrn_lb, hgrn_onorm_g, s5_lambda_re, s5_lambda_im,
                     s5_log_dt, s5_b_re, s5_b_im, s5_c_re, s5_c_im, s5_d, s5_glu_w, s5_glu_b,
                     s5_merge_g, w_out, norm2_g, w_ff1, w_ff2, norm_f_g)
    y_sample = trunk(x_sample, norm1_g, w_in, hgrn_lb, hgrn_onorm_g, s5_lambda_re, s5_lambda_im,
                     s5_log_dt, s5_b_re, s5_b_im, s5_c_re, s5_c_im, s5_d, s5_glu_w, s5_glu_b,
                     s5_merge_g, w_out, norm2_g, w_ff1, w_ff2, norm_f_g)
    return (y_prompt, y_sample)
```

```python
import numpy as np
import ml_dtypes
import concourse.bass as bass
import concourse.mybir as mybir
from concourse.bass_utils import run_bass_kernel_spmd

F32 = mybir.dt.float32
BF16 = mybir.dt.bfloat16
I32 = mybir.dt.int32
AF = mybir.ActivationFunctionType
ALU = mybir.AluOpType
AX = mybir.AxisListType

NCORES = 8
D = 1024
DFF = 4096
SEG = 2048
NSEG = 5
EPS = 1e-6


class Buf:
    __slots__ = ("name", "w", "r", "dsem", "dall", "dwr")

    def __init__(self, name):
        self.name = name
        self.w = None
        self.r = {}
        self.dsem = None
        self.dall = set()
        self.dwr = set()


class Sch:
    ENGS = ("pe", "act", "dve", "pool", "sp")

    def __init__(self, nc):
        self.nc = nc
        self.ops = {e: [] for e in self.ENGS}
        self.cnt = {e: 0 for e in self.ENGS}
        self.known = {e: {} for e in self.ENGS}
        self.semkeys = []
        self.semcount = {}
        self.seminc = {}
        for e in self.ENGS:
            self.semkeys.append(("eng", e))
        self.nbuf = 0
        self.allbufs = []

    def barrier(self):
        for e in self.ENGS:
            waits = {}
            for e2 in ("pe", "act", "dve", "pool"):
                if e2 != e and self.cnt[e2] > 0:
                    self._need(e, waits, ("eng", e2), self.cnt[e2])
            for k, c in self.semcount.items():
                if c > 0:
                    self._need(e, waits, k, self.seminc.get(k, 16) * c)
            for k, v in waits.items():
                self.known[e][k] = v
            self.ops[e].append((list(waits.items()), None, []))
        for b in self.allbufs:
            b.w = None
            b.r = {}
            b.dwr = set()

    def barrier_local(self, skip=()):
        for e in self.ENGS:
            waits = {}
            for e2 in ("pe", "act", "dve", "pool"):
                if e2 != e and self.cnt[e2] > 0:
                    self._need(e, waits, ("eng", e2), self.cnt[e2])
            for k, c in self.semcount.items():
                if c > 0 and self.seminc.get(k, 16) == 16:
                    self._need(e, waits, k, 16 * c)
            for k, v in waits.items():
                self.known[e][k] = v
            self.ops[e].append((list(waits.items()), None, []))
        for b in self.allbufs:
            if self.seminc.get(b.dsem, 16) == 16 and not any(self.seminc.get(k, 16) != 16 for k in b.dwr):
                b.w = None
                b.r = {}
                b.dwr = set()

    def buf(self, name):
        self.nbuf += 1
        b = Buf(f"{name}#{self.nbuf}")
        self.allbufs.append(b)
        return b

    def bufs(self, name, n):
        return [self.buf(f"{name}{i}") for i in range(n)]

    def _need(self, eng, waits, key, val):
        if self.known[eng].get(key, 0) >= val:
            return
        if waits.get(key, 0) < val:
            waits[key] = val

    def _dsem(self, b):
        if b.dsem is None:
            b.dsem = ("dma", b.name)
            self.semkeys.append(b.dsem)
            self.semcount[b.dsem] = 0
        return b.dsem

    def _dneed(self, eng, waits, keys):
        for k in keys:
            self._need(eng, waits, k, self.seminc.get(k, 16) * self.semcount[k])

    def op(self, eng, fn, reads=(), writes=()):
        waits = {}
        for b in reads:
            if b.w is not None and not (b.w[0] == eng and eng == "pe"):
                self._need(eng, waits, ("eng", b.w[0]), b.w[1])
            self._dneed(eng, waits, b.dwr)
        for b in writes:
            if b.w is not None and (b.w[0] != eng or eng != "pe"):
                self._need(eng, waits, ("eng", b.w[0]), b.w[1])
            for e2, i2 in b.r.items():
                if e2 != eng or eng != "pe":
                    self._need(eng, waits, ("eng", e2), i2)
            self._dneed(eng, waits, b.dall)
        for k, v in waits.items():
            self.known[eng][k] = v
        self.cnt[eng] += 1
        idx = self.cnt[eng]
        self.ops[eng].append((list(waits.items()), fn, [(("eng", eng), 1)]))
        for b in writes:
            b.w = (eng, idx)
            b.r = {}
            b.dwr = set()
        for b in reads:
            if b.r.get(eng, 0) < idx:
                b.r[eng] = idx
        return idx

    def coll(self, fn, reads, writes):
        self.dma("pool", fn, reads=reads, writes=writes, inc=1)

    def dma(self, q, fn, reads=(), writes=(), inc=16):
        waits = {}
        for b in reads:
            if b.w is not None:
                self._need(q, waits, ("eng", b.w[0]), b.w[1])
            self._dneed(q, waits, b.dwr)
        for b in writes:
            if b.w is not None:
                self._need(q, waits, ("eng", b.w[0]), b.w[1])
            for e2, i2 in b.r.items():
                self._need(q, waits, ("eng", e2), i2)
            self._dneed(q, waits, b.dall)
        for k, v in waits.items():
            self.known[q][k] = v
        prim = writes[0] if len(writes) else reads[0]
        key = self._dsem(prim)
        self.seminc[key] = inc
        self.semcount[key] += 1
        for b in list(reads) + list(writes):
            b.dall.add(key)
        for b in writes:
            b.dwr = {key}
            b.w = None
            b.r = {}
        self.ops[q].append((list(waits.items()), fn, [(key, inc)]))

    def finish(self, q, bufs):
        waits = {}
        for b in bufs:
            self._dneed(q, waits, b.dall)
        self.ops[q].append((list(waits.items()), None, []))

    def replay(self, stack):
        nc = self.nc
        sems = {}
        for i, k in enumerate(self.semkeys):
            sems[k] = stack.enter_context(nc.semaphore(f"sm{i}"))
        block = stack.enter_context(nc.Block())
        sch = self

        def run(engname, eng):
            for waits, fn, incs in sch.ops[engname]:
                for k, v in waits:
                    eng.wait_ge(sems[k], v)
                if fn is None:
                    continue
                inst = fn(eng)
                for k, v in incs:
                    inst = inst.then_inc(sems[k], v)

        @block.tensor
        def _(e):
            run("pe", e)

        @block.scalar
        def _(e):
            run("act", e)

        @block.vector
        def _(e):
            run("dve", e)

        @block.gpsimd
        def _(e):
            run("pool", e)

        @block.sync
        def _(e):
            run("sp", e)


def _dsize(dt):
    return 4 if dt in (F32, I32) else 2


class Arena:
    def __init__(self, tens, ncols):
        self.t = tens
        self.n = ncols
        self.off = 0

    def reset(self, off=0):
        self.off = off

    def alloc(self, free_shape, dt):
        nel = int(np.prod(free_shape))
        ncol = (nel * _dsize(dt) + 3) // 4
        ncol = (ncol + 7) // 8 * 8
        assert self.off + ncol <= self.n, f"arena overflow: need {self.off + ncol} have {self.n}"
        ap = self.t[:, self.off:self.off + ncol]
        self.off += ncol
        if dt != F32:
            ap = ap.bitcast(dt)
        ap = ap[:, 0:nel]
        if len(free_shape) > 1:
            names = [f"d{i}" for i in range(len(free_shape))]
            pat = "p (" + " ".join(names) + ") -> p " + " ".join(names)
            ap = ap.rearrange(pat, **{n: int(v) for n, v in zip(names[:-1], free_shape[:-1])})
        return ap


class Ctx:
    pass


def phase_F(C, ntok):
    S, A, PS, nc = C.S, C.arena, C.psum, C.nc
    A.reset()
    TT = 256
    nt = ntok // TT
    if getattr(C, 'dbg_nt', None) is not None:
        nt = C.dbg_nt
    w1b = A.alloc([8, DFF], BF16)
    w2b = A.alloc([32, D], BF16)
    stg = [A.alloc([2048], F32) for _ in range(2)]
    abuf = A.alloc([32, TT], BF16)
    x1t = [A.alloc([2, D], F32) for _ in range(2)]
    x2t = A.alloc([D], F32)
    outt = [A.alloc([D], F32) for _ in range(2)]
    h2 = A.alloc([D], BF16)
    h2T = A.alloc([8, TT], BF16)
    sq = [A.alloc([TT], F32) for _ in range(2)]
    gF = A.alloc([D], F32)
    g2c = A.alloc([8], F32)
    junk = A.alloc([D], BF16)
    st = A.alloc([16], F32)
    ident = C.ident_bf

    B_w1 = S.bufs("w1b", 8)
    B_w2 = S.bufs("w2b", 32)
    B_stg = S.bufs("stg", 2)
    B_a = S.bufs("a", 32)
    B_x1 = S.bufs("x1t", 2)
    B_x2 = S.buf("x2t")
    B_out = S.bufs("outt", 2)
    B_h2 = S.buf("h2")
    B_h2T = S.buf("h2T")
    B_sq = S.bufs("sq", 2)
    B_gF = S.buf("gF")
    B_g2 = S.buf("g2c")
    B_junk = S.buf("junk")
    B_st = S.buf("st")
    B_ps = C.B_ps
    B_id = C.B_ident

    S.dma("sp", lambda e: e.dma_start(out=gF, in_=C.d["norm_f_g"].partition_broadcast(128)), writes=[B_gF])
    S.dma("sp", lambda e: e.dma_start(out=g2c, in_=C.d["norm2_g"].rearrange("(k p) -> p k", p=128),
                                      allow_slow_non_contiguous=True), writes=[B_g2])
    w1d = C.d["w_ff1"]
    n = 0
    for k in range(8):
        for hh in range(2):
            sl = n % 2
            S.dma("sp", lambda e, k=k, hh=hh, sl=sl: e.dma_start(
                out=stg[sl], in_=w1d[k * 128:(k + 1) * 128, hh * 2048:(hh + 1) * 2048]), writes=[B_stg[sl]])
            if n % 2 == 0:
                S.op("act", lambda e, k=k, hh=hh, sl=sl: e.activation(
                    out=w1b[:, k, hh * 2048:(hh + 1) * 2048], in_=stg[sl], func=AF.Copy, scale=g2c[:, k:k + 1]),
                    reads=[B_stg[sl], B_g2], writes=[B_w1[k]])
            else:
                S.op("dve", lambda e, k=k, hh=hh, sl=sl: e.tensor_scalar(
                    out=w1b[:, k, hh * 2048:(hh + 1) * 2048], in0=stg[sl], scalar1=g2c[:, k:k + 1], scalar2=None,
                    op0=ALU.mult), reads=[B_stg[sl], B_g2], writes=[B_w1[k]])
            n += 1
    w2d = C.d["w_ff2"].rearrange("(m p) n -> p m n", p=128)
    for m2 in range(16):
        sl = n % 2
        S.dma("sp", lambda e, m2=m2, sl=sl: e.dma_start(
            out=stg[sl].rearrange("p (a b) -> p a b", a=2), in_=w2d[:, 2 * m2:2 * m2 + 2, :]), writes=[B_stg[sl]])
        eng = "act" if n % 2 == 0 else "dve"
        if eng == "act":
            S.op("act", lambda e, m2=m2, sl=sl: e.activation(
                out=w2b[:, 2 * m2:2 * m2 + 2, :], in_=stg[sl].rearrange("p (a b) -> p a b", a=2), func=AF.Copy),
                reads=[B_stg[sl]], writes=[B_w2[2 * m2], B_w2[2 * m2 + 1]])
        else:
            S.op("dve", lambda e, m2=m2, sl=sl: e.tensor_copy(
                out=w2b[:, 2 * m2:2 * m2 + 2, :], in_=stg[sl].rearrange("p (a b) -> p a b", a=2)),
                reads=[B_stg[sl]], writes=[B_w2[2 * m2], B_w2[2 * m2 + 1]])
        n += 1

    x1d = C.d["x1"].rearrange("(t s p) f -> t p s f", s=2, p=128)
    outd = C.d["out"].rearrange("(t s p) f -> t s p f", s=2, p=128)

    def rstd_from_ss(ss_ap, out_ap):
        S.op("dve", lambda e: e.tensor_scalar(out=out_ap, in0=ss_ap, scalar1=1.0 / D, scalar2=EPS,
                                              op0=ALU.mult, op1=ALU.add), reads=[B_st], writes=[B_st])
        S.op("act", lambda e: e.activation(out=out_ap, in_=out_ap, func=AF.Sqrt), reads=[B_st], writes=[B_st])
        S.op("dve", lambda e: e.reciprocal(out=out_ap, in_=out_ap), reads=[B_st], writes=[B_st])

    for t in range(nt):
        xs = t % 2
        S.dma("sp", lambda e, t=t, xs=xs: e.dma_start(out=x1t[xs], in_=x1d[t]), writes=[B_x1[xs]])
        for s_ in range(2):
            S.op("act", lambda e, xs=xs, s_=s_: e.activation(out=junk, in_=x1t[xs][:, s_, :], func=AF.Square,
                                                            accum_out=st[:, 0:1]),
                 reads=[B_x1[xs]], writes=[B_junk, B_st])
            rstd_from_ss(st[:, 0:1], st[:, 1:2])
            S.op("dve", lambda e, xs=xs, s_=s_: e.tensor_scalar(out=h2, in0=x1t[xs][:, s_, :], scalar1=st[:, 1:2],
                                                               scalar2=None, op0=ALU.mult),
                 reads=[B_x1[xs], B_st], writes=[B_h2])
            pb = 6 + (s_ % 2)
            pst = PS[:, pb, :].bitcast(BF16)
            for k in range(8):
                S.op("pe", lambda e, k=k, pst=pst: e.transpose(out=pst[:, k * 128:(k + 1) * 128],
                                                               in_=h2[:, k * 128:(k + 1) * 128], identity=ident),
                     reads=[B_h2, B_id], writes=[B_ps[pb]])
            S.op("act", lambda e, s_=s_, pst=pst: e.activation(
                out=h2T[:, :, s_ * 128:(s_ + 1) * 128], in_=pst.rearrange("p (k n) -> p k n", k=8), func=AF.Copy),
                reads=[B_ps[pb]], writes=[B_h2T])
        if getattr(C, 'dbg_cut', 9) <= 1:
            continue
        for m in range(32):
            pb = m % 2
            po = PS[:, pb, 0:TT]
            for k in range(8):
                S.op("pe", lambda e, m=m, k=k, po=po: e.matmul(po, lhsT=w1b[:, k, m * 128:(m + 1) * 128],
                                                               rhs=h2T[:, k, :], start=(k == 0), stop=(k == 7)),
                     reads=[B_w1[k], B_h2T], writes=[B_ps[pb]])
            S.op("act", lambda e, pb=pb, po=po: e.activation(out=sq[pb], in_=po, func=AF.Square),
                 reads=[B_ps[pb]], writes=[B_sq[pb]])
            S.op("dve", lambda e, m=m, pb=pb, po=po: e.scalar_tensor_tensor(
                out=abuf[:, m, :], in0=po, scalar=0.0, in1=sq[pb], op0=ALU.is_gt, op1=ALU.mult),
                reads=[B_ps[pb], B_sq[pb]], writes=[B_a[m]])
        if getattr(C, 'dbg_cut', 9) <= 2:
            continue
        for s_ in range(2):
            for hf in range(2):
                pb = 2 + 2 * (s_ % 2) + hf
                po = PS[:, pb, :]
                for m in range(32):
                    S.op("pe", lambda e, m=m, s_=s_, hf=hf, po=po: e.matmul(
                        po, lhsT=abuf[:, m, s_ * 128:(s_ + 1) * 128], rhs=w2b[:, m, hf * 512:(hf + 1) * 512],
                        start=(m == 0), stop=(m == 31)), reads=[B_a[m], B_w2[m]], writes=[B_ps[pb]])
                S.op("dve", lambda e, xs=xs, s_=s_, hf=hf, po=po: e.tensor_tensor(
                    out=x2t[:, hf * 512:(hf + 1) * 512], in0=po, in1=x1t[xs][:, s_, hf * 512:(hf + 1) * 512],
                    op=ALU.add), reads=[B_ps[pb], B_x1[xs]], writes=[B_x2])
            S.op("act", lambda e: e.activation(out=junk, in_=x2t, func=AF.Square, accum_out=st[:, 2:3]),
                 reads=[B_x2], writes=[B_junk, B_st])
            rstd_from_ss(st[:, 2:3], st[:, 3:4])
            os_ = (2 * t + s_) % 2
            S.op("dve", lambda e, os_=os_: e.scalar_tensor_tensor(out=outt[os_], in0=x2t, scalar=st[:, 3:4], in1=gF,
                                                                 op0=ALU.mult, op1=ALU.mult),
                 reads=[B_x2, B_st, B_gF], writes=[B_out[os_]])
            S.dma("sp", lambda e, t=t, s_=s_, os_=os_: e.dma_start(out=outd[t, s_], in_=outt[os_]),
                  reads=[B_out[os_]])
    C.final_bufs = list(B_out)


ARENA_COLS = 52672

WEIGHT_SHAPES = {
    "norm1_g": [D], "w_in": [D, 3072], "hgrn_lb": [2, 2, 512], "hgrn_onorm_g": [512],
    "s5_lambda_re": [2, 32, 64], "s5_lambda_im": [2, 32, 64], "s5_log_dt": [2, 32],
    "s5_b_re": [2, 32, 64, 16], "s5_b_im": [2, 32, 64, 16], "s5_c_re": [2, 32, 16, 64], "s5_c_im": [2, 32, 16, 64],
    "s5_d": [512], "s5_glu_w": [512, 512], "s5_glu_b": [512], "s5_merge_g": [512],
    "w_out": [D, D], "norm2_g": [D], "w_ff1": [D, DFF], "w_ff2": [DFF, D], "norm_f_g": [D],
}


def build(ntok, phases="SHF", test_inputs=(), test_outputs=(), dbg=None):
    from contextlib import ExitStack
    nc = bass.Bass("TRN2", target_bir_lowering=False, num_devices=NCORES)
    C = Ctx()
    C.nc = nc
    C.d = {}
    C.ntok = ntok
    C.seglen = SEG
    for k_, v_ in (dbg or {}).items():
        setattr(C, k_, v_)
    C.d["x"] = nc.dram_tensor("x", [ntok, D], F32, kind="ExternalInput").ap()
    for k, shp in WEIGHT_SHAPES.items():
        C.d[k] = nc.dram_tensor(k, shp, F32, kind="ExternalInput").ap()
    C.d["ident"] = nc.dram_tensor("ident", [128, 128], BF16, kind="ExternalInput").ap()
    C.d["negmask"] = nc.dram_tensor("negmask", [2, 128, 128], F32, kind="ExternalInput").ap()
    C.d["cmask"] = nc.dram_tensor("cmask", [128, 1024], F32, kind="ExternalInput").ap()
    C.d["identf"] = nc.dram_tensor("identf", [128, 128], F32, kind="ExternalInput").ap()
    C.d["mh"] = nc.dram_tensor("mh", [128, 2], F32, kind="ExternalInput").ap()
    C.d["i32"] = nc.dram_tensor("i32", [128, 32], F32, kind="ExternalInput").ap()
    C.d["m1_scr"] = nc.dram_tensor("m1_scr", [128, 24576], BF16, kind="Internal").ap()
    C.d["g_scr"] = nc.dram_tensor("g_scr", [128, 18432], BF16, kind="Internal").ap()
    C.d["e_scr"] = nc.dram_tensor("e_scr", [128, 8192], F32, kind="Internal").ap()
    C.d["oh"] = nc.dram_tensor("oh", [128, NCORES], F32, kind="ExternalInput").ap()
    C.d["exs_src"] = nc.dram_tensor("exs_src", [128, 64], F32, kind="Internal").ap()
    C.d["exs_dst"] = nc.dram_tensor("exs_dst", [128 * NCORES, 64], F32, kind="Internal").ap()
    for k_ in range(8):
        C.d[f"exh_src{k_}"] = nc.dram_tensor(f"exh_src{k_}", [128, 129], F32, kind="Internal").ap()
        C.d[f"exh_dst{k_}"] = nc.dram_tensor(f"exh_dst{k_}", [128 * NCORES, 129], F32, kind="Internal").ap()
    C.exchange = getattr(C, "exchange", True)
    C.d["out"] = nc.dram_tensor("out", [ntok, D], F32, kind="ExternalOutput").ap()
    scratch = {"x1": ([ntok, D], F32), "yb": ([ntok, 512], BF16)}
    for k, (shp, dt) in scratch.items():
        if k in test_inputs:
            C.d[k] = nc.dram_tensor(k, shp, dt, kind="ExternalInput").ap()
        elif k in test_outputs:
            C.d[k] = nc.dram_tensor(k, shp, dt, kind="ExternalOutput").ap()
        else:
            C.d[k] = nc.dram_tensor(k, shp, dt, kind="Internal").ap()
    with ExitStack() as stack:
        arena_t = stack.enter_context(nc.sbuf_tensor("arena", [128, ARENA_COLS], F32))
        ident_t = stack.enter_context(nc.sbuf_tensor("identbf", [128, 128], BF16))
        identf_t = stack.enter_context(nc.sbuf_tensor("identf32", [128, 128], F32))
        psum_t = stack.enter_context(nc.psum_tensor("psum", [128, 8, 512], F32))
        S = Sch(nc)
        C.S = S
        C.arena = Arena(arena_t, ARENA_COLS)
        C.psum = psum_t
        C.ident_bf = ident_t[:, :]
        C.B_ident = S.buf("ident")
        C.B_ps = S.bufs("psum", 8)
        C.final_bufs = []
        S.dma("sp", lambda e: e.dma_start(out=C.ident_bf, in_=C.d["ident"]), writes=[C.B_ident])
        C.ident_f = identf_t[:, :]
        C.B_identf = S.buf("identf")
        S.dma("sp", lambda e: e.dma_start(out=C.ident_f, in_=C.d["identf"]), writes=[C.B_identf])
        for ph in phases:
            if ph == "F":
                phase_F(C, ntok)
            if ph == "H":
                phase_H(C, ntok)
            if ph == "S":
                phase_S(C, ntok)
            S.barrier()
        S.finish("sp", C.final_bufs)
        S.replay(stack)
    return nc


def host_consts():
    s = np.arange(128)[:, None]
    t = np.arange(128)[None, :]
    same = (s // 64) == (t // 64)
    negmask = np.stack([-(same & (s <= t)).astype(np.float32), -(same & (s >= t)).astype(np.float32)])
    cm = np.ones((128, 1024), np.float32)
    cm[:, 0::64] = 0.0
    q = np.arange(128)
    mh = np.stack([((q // 16) % 2 == 0), ((q // 16) % 2 == 1)], axis=1).astype(np.float32)
    i32 = (np.arange(32)[None, :] == (q % 32)[:, None]).astype(np.float32)
    return {"ident": np.eye(128, dtype=np.float32).astype(ml_dtypes.bfloat16), "negmask": negmask, "cmask": cm,
            "identf": np.eye(128, dtype=np.float32), "mh": mh, "i32": i32}


def phase_H(C, ntok):
    S, A, PS, nc = C.S, C.arena, C.psum, C.nc
    A.reset()
    seglen = C.seglen
    nseg = ntok // seglen
    TT = 256
    tps = seglen // TT
    npair = seglen // 128
    ident = C.ident_bf
    B_id = C.B_ident
    B_ps = C.B_ps

    winb = A.alloc([8, 2560], BF16)
    woutb = A.alloc([8, D], BF16)
    negm = A.alloc([2, 128], F32)
    cmask = A.alloc([1024], F32)
    lbraw = A.alloc([2, 2, 4], F32)
    lb = A.alloc([2, 4], F32)
    oml = A.alloc([2, 4], F32)
    g1c = A.alloc([8], F32)
    onormg = A.alloc([512], F32)
    B_win = S.bufs("winb", 8)
    B_wout = S.bufs("woutb", 8)
    B_cst = S.buf("Hconst")
    mark = A.off
    stg = [A.alloc([2048], F32) for _ in range(2)]
    B_stg = S.bufs("Hstg", 2)

    S.dma("sp", lambda e: e.dma_start(out=negm, in_=C.d["negmask"].rearrange("d s t -> s d t")), writes=[B_cst])
    S.dma("sp", lambda e: e.dma_start(out=cmask, in_=C.d["cmask"]), writes=[B_cst])
    S.dma("sp", lambda e: e.dma_start(out=lbraw, in_=C.d["hgrn_lb"].rearrange("d l (h p) -> p d l h", p=128),
                                      allow_slow_non_contiguous=True), writes=[B_cst])
    S.dma("sp", lambda e: e.dma_start(out=g1c, in_=C.d["norm1_g"].rearrange("(k p) -> p k", p=128),
                                      allow_slow_non_contiguous=True), writes=[B_cst])
    S.dma("sp", lambda e: e.dma_start(out=onormg, in_=C.d["hgrn_onorm_g"].partition_broadcast(128)), writes=[B_cst])
    S.op("dve", lambda e: e.tensor_tensor(out=lb, in0=lbraw[:, :, 0, :], in1=lbraw[:, :, 1, :], op=ALU.subtract),
         reads=[B_cst], writes=[B_cst])
    S.op("act", lambda e: e.activation(out=lb, in_=lb, func=AF.Sigmoid), reads=[B_cst], writes=[B_cst])
    S.op("dve", lambda e: e.tensor_scalar(out=oml, in0=lb, scalar1=-1.0, scalar2=1.0, op0=ALU.mult, op1=ALU.add),
         reads=[B_cst], writes=[B_cst])
    wind = C.d["w_in"]
    n = 0
    for k in range(8):
        for (c0, c1, o0) in ((0, 2048, 0), (2048, 2560, 2048)):
            sl = n % 2
            w = c1 - c0
            S.dma("sp", lambda e, k=k, c0=c0, c1=c1, sl=sl, w=w: e.dma_start(
                out=stg[sl][:, 0:w], in_=wind[k * 128:(k + 1) * 128, c0:c1]), writes=[B_stg[sl]])
            if n % 2 == 0:
                S.op("act", lambda e, k=k, o0=o0, sl=sl, w=w: e.activation(
                    out=winb[:, k, o0:o0 + w], in_=stg[sl][:, 0:w], func=AF.Copy, scale=g1c[:, k:k + 1]),
                    reads=[B_stg[sl], B_cst], writes=[B_win[k]])
            else:
                S.op("dve", lambda e, k=k, o0=o0, sl=sl, w=w: e.tensor_scalar(
                    out=winb[:, k, o0:o0 + w], in0=stg[sl][:, 0:w], scalar1=g1c[:, k:k + 1], scalar2=None,
                    op0=ALU.mult), reads=[B_stg[sl], B_cst], writes=[B_win[k]])
            n += 1
    woutd = C.d["w_out"]
    for k in range(8):
        sl = n % 2
        S.dma("sp", lambda e, k=k, sl=sl: e.dma_start(out=stg[sl][:, 0:D], in_=woutd[k * 128:(k + 1) * 128, :]),
              writes=[B_stg[sl]])
        if n % 2 == 0:
            S.op("act", lambda e, k=k, sl=sl: e.activation(out=woutb[:, k, :], in_=stg[sl][:, 0:D], func=AF.Copy),
                 reads=[B_stg[sl]], writes=[B_wout[k]])
        else:
            S.op("dve", lambda e, k=k, sl=sl: e.tensor_copy(out=woutb[:, k, :], in_=stg[sl][:, 0:D]),
                 reads=[B_stg[sl]], writes=[B_wout[k]])
        n += 1
    S.barrier()
    A.reset(mark)

    of = A.alloc([npair, 512], F32)
    Gst = A.alloc([npair, 512], BF16)
    xt = [A.alloc([2, D], F32) for _ in range(2)]
    hbf = A.alloc([D], BF16)
    hT = A.alloc([8, TT], BF16)
    qsb = A.alloc([4, TT], F32)
    fg = A.alloc([4, TT], F32)
    lf = A.alloc([4, TT], F32)
    bc = A.alloc([4, TT], F32)
    eb2 = [A.alloc([4, TT], F32) for _ in range(2)]
    enb = lf
    qe2 = [A.alloc([4, TT], BF16) for _ in range(2)]
    nke2 = [A.alloc([4, TT], BF16) for _ in range(2)]
    nkeT2 = [A.alloc([2, 4, 128], BF16) for _ in range(2)]
    vbf2 = [A.alloc([2, 512], BF16) for _ in range(2)]
    attm = [A.alloc([4, 128], BF16) for _ in range(2)]
    Sst = A.alloc([4, 128], F32)
    Stmp = A.alloc([4, 128], F32)
    Sbf = A.alloc([4, 128], BF16)
    tmpg = A.alloc([512], F32)
    osum = A.alloc([512], F32)
    ya = A.alloc([512], BF16)
    yaT = A.alloc([4, 128], BF16)
    ybt = [A.alloc([512], BF16) for _ in range(2)]
    ybTs = [A.alloc([4, 128], BF16) for _ in range(2)]
    x1o = [A.alloc([D], F32) for _ in range(2)]
    junk = A.alloc([D], BF16)
    st1 = A.alloc([4], F32)
    st2 = A.alloc([8], F32)
    Ltot = A.alloc([4], F32)
    Lt1 = A.alloc([4], F32)
    ohs = A.alloc([NCORES], F32)
    off_q = qsb.rearrange("p h t -> p (h t)").offset
    off_f = fg.rearrange("p h t -> p (h t)").offset
    off_l = lf.rearrange("p h t -> p (h t)").offset
    off_b = bc.rearrange("p h t -> p (h t)").offset
    assert off_f == off_q + 1024 and off_l == off_f + 1024 and off_b == off_l + 1024
    EXH = A.t[:, off_f:off_f + 1032]
    Smine = A.t[:, off_l + 512:off_l + 1024].rearrange("p (h n) -> p h n", h=4)
    Sin = A.t[:, off_b:off_b + 512].rearrange("p (h n) -> p h n", h=4)
    grb = [A.t[:, off_q:off_q + 516], A.t[:, off_b + 512:off_b + 512 + 516]]
    B_exh = S.buf("Hex")
    B_grb = S.bufs("grb", 2)
    B_exhsrc = S.bufs("exh_src", 8)
    B_exhdst = S.bufs("exh_dst", 8)

    B_of = S.bufs("of", npair)
    B_G = S.bufs("Gst", npair)
    B_xt = S.bufs("Hxt", 2)
    B_hbf = S.buf("hbf")
    B_hT = S.buf("hT")
    B_q = S.buf("qsb")
    B_f = S.buf("fg")
    B_lf = S.buf("lf")
    B_bc = S.buf("bc")
    B_eb2 = S.bufs("eb", 2)
    B_enb = B_lf
    B_qe2 = S.bufs("qe", 2)
    B_nke2 = S.bufs("nke", 2)
    B_nkeT2 = [S.bufs(f"nkeT{i}_", 2) for i in range(2)]
    B_v2 = [S.bufs(f"vbf{i}_", 2) for i in range(2)]
    B_attm = [S.bufs(f"attm{i}_", 4) for i in range(2)]
    B_S = S.bufs("Sst", 4)
    B_Stmp = S.bufs("Stmp", 4)
    B_Sbf = S.bufs("Sbf", 4)
    B_tmpg = S.buf("tmpg")
    B_osum = S.buf("osum")
    B_ya = S.buf("ya")
    B_yaT = S.buf("yaT")
    B_ybt = S.bufs("ybt", 2)
    B_ybTs = S.bufs("ybTs", 2)
    B_x1o = S.bufs("x1o", 2)
    B_junk = S.buf("Hjunk")
    B_st1 = S.buf("st1")
    B_st2 = S.buf("st2")
    B_pin = [B_ps[0], B_ps[1]]
    B_patt = [B_ps[4]] * 4
    B_psu = [B_ps[6]] * 4
    pin = [PS[:, 0, 0:256], PS[:, 1, 0:256]]
    patt = [PS[:, 4, h * 128:(h + 1) * 128] for h in range(4)]
    psu = [PS[:, 6, h * 128:(h + 1) * 128] for h in range(4)]
    pso = PS[:, 5, :]
    ps7 = PS[:, 7, :].bitcast(BF16)

    xd = C.d["x"]
    x1d = C.d["x1"]
    ybd = C.d["yb"]
    lb_b = [lb[:, d, :].rearrange("p (h o) -> p h o", o=1).to_broadcast([128, 4, TT]) for d in range(2)]
    oml_b = [oml[:, d, :].rearrange("p (h o) -> p h o", o=1).to_broadcast([128, 4, TT]) for d in range(2)]
    cm = cmask.rearrange("p (h t) -> p h t", h=4)
    fl = lambda ap: ap.rearrange("p h t -> p (h t)")
    cnt = {"x": 0, "pin": 0, "attm": 0, "x1o": 0, "yb": 0, "tile": 0}

    def rstd_op(B, ss_ap, out_ap, inv_n):
        S.op("dve", lambda e: e.tensor_scalar(out=out_ap, in0=ss_ap, scalar1=inv_n, scalar2=EPS,
                                              op0=ALU.mult, op1=ALU.add), reads=[B], writes=[B])
        S.op("act", lambda e: e.activation(out=out_ap, in_=out_ap, func=AF.Sqrt), reads=[B], writes=[B])
        S.op("dve", lambda e: e.reciprocal(out=out_ap, in_=out_ap), reads=[B], writes=[B])

    def inproj_fm(col0, evac):
        sl = cnt["pin"] % 2
        cnt["pin"] += 1
        for k in range(8):
            S.op("pe", lambda e, k=k, sl=sl: e.matmul(pin[sl], lhsT=winb[:, k, col0:col0 + 128], rhs=hT[:, k, :],
                                                      start=(k == 0), stop=(k == 7)),
                 reads=[B_win[k], B_hT], writes=[B_pin[sl]])
        evac(pin[sl], B_pin[sl])

    def prologue(seg, d, ti, state_only, tp, info):
        eb, qe, nke, nkeT, vbf = eb2[tp], qe2[tp], nke2[tp], nkeT2[tp], vbf2[tp]
        B_eb, B_qe, B_nke, B_nkeT, B_v = B_eb2[tp], B_qe2[tp], B_nke2[tp], B_nkeT2[tp], B_v2[tp]
        tok0 = seg * seglen + ti * TT
        xs = cnt["x"] % 2
        cnt["x"] += 1
        info["xs"] = xs
        info["tok0"] = tok0
        S.dma("sp", lambda e, tok0=tok0, xs=xs: e.dma_start(
            out=xt[xs], in_=xd[tok0:tok0 + TT, :].rearrange("(s p) f -> p s f", p=128)), writes=[B_xt[xs]])
        for sub in range(2):
            S.op("act", lambda e, xs=xs, sub=sub: e.activation(out=junk, in_=xt[xs][:, sub, :], func=AF.Square,
                                                             accum_out=st1[:, 0:1]),
                 reads=[B_xt[xs]], writes=[B_junk, B_st1])
            rstd_op(B_st1, st1[:, 0:1], st1[:, 1:2], 1.0 / D)
            S.op("dve", lambda e, xs=xs, sub=sub: e.tensor_scalar(out=hbf, in0=xt[xs][:, sub, :],
                                                                scalar1=st1[:, 1:2], scalar2=None, op0=ALU.mult),
                 reads=[B_xt[xs], B_st1], writes=[B_hbf])
            for k in range(8):
                S.op("pe", lambda e, k=k: e.transpose(out=ps7[:, k * 128:(k + 1) * 128],
                                                      in_=hbf[:, k * 128:(k + 1) * 128], identity=ident),
                     reads=[B_hbf, B_id], writes=[B_ps[7]])
            S.op("act", lambda e, sub=sub: e.activation(
                out=hT[:, :, sub * 128:(sub + 1) * 128], in_=ps7.rearrange("p (k n) -> p k n", k=8), func=AF.Copy),
                reads=[B_ps[7]], writes=[B_hT])
            yield
        yield
        for h in (range(4) if not state_only else ()):
            inproj_fm(h * 128, lambda ap, B, h=h: S.op(
                "act", lambda e: e.activation(out=qsb[:, h, :], in_=ap, func=AF.Copy), reads=[B], writes=[B_q]))
            yield
        for h in range(4):
            inproj_fm(512 * (1 + d) + h * 128, lambda ap, B, h=h: S.op(
                "act", lambda e: e.activation(out=fg[:, h, :], in_=ap, func=AF.Sigmoid), reads=[B], writes=[B_f]))
            yield
        for sub in range(2):
            for k in range(8):
                S.op("pe", lambda e, k=k, sub=sub: e.matmul(PS[:, 2, :], lhsT=hT[:, k, sub * 128:(sub + 1) * 128],
                                                            rhs=winb[:, k, 1536:2048], start=(k == 0), stop=(k == 7)),
                     reads=[B_hT, B_win[k]], writes=[B_ps[2]])
            S.op("dve", lambda e, sub=sub: e.tensor_copy(out=vbf[:, sub, :], in_=PS[:, 2, :]),
                 reads=[B_ps[2]], writes=[B_v[sub]])
            yield
            if d == 0 and not state_only:
                pidx = ti * 2 + sub
                for k in range(8):
                    S.op("pe", lambda e, k=k, sub=sub: e.matmul(PS[:, 3, :], lhsT=hT[:, k, sub * 128:(sub + 1) * 128],
                                                                rhs=winb[:, k, 2048:2560], start=(k == 0), stop=(k == 7)),
                         reads=[B_hT, B_win[k]], writes=[B_ps[3]])
                S.op("act", lambda e: e.activation(out=tmpg, in_=PS[:, 3, :], func=AF.Silu),
                     reads=[B_ps[3]], writes=[B_tmpg])
                S.op("pool", lambda e, pidx=pidx: e.tensor_tensor(out=Gst[:, pidx, :], in0=tmpg, in1=onormg, op=ALU.mult),
                     reads=[B_tmpg, B_cst], writes=[B_G[pidx]])
        yield
        S.op("dve", lambda e, d=d: e.tensor_tensor(out=fg, in0=fg, in1=oml_b[d], op=ALU.mult),
             reads=[B_f, B_cst], writes=[B_f])
        S.op("dve", lambda e, d=d: e.tensor_tensor(out=fg, in0=fg, in1=lb_b[d], op=ALU.add),
             reads=[B_f, B_cst], writes=[B_f])
        S.op("act", lambda e: e.activation(out=lf, in_=fg, func=AF.Ln), reads=[B_f], writes=[B_lf])
        yield
        if d == 0:
            S.op("dve", lambda e: e.tensor_tensor_scan(out=fl(bc), data0=cmask, data1=fl(lf), initial=0.0,
                                                       op0=ALU.mult, op1=ALU.add),
                 reads=[B_lf, B_cst], writes=[B_bc])
        else:
            S.op("dve", lambda e: e.tensor_tensor_scan(out=fl(bc)[:, ::-1], data0=cmask[:, :], data1=fl(lf)[:, ::-1],
                                                       initial=0.0, op0=ALU.mult, op1=ALU.add),
                 reads=[B_lf, B_cst], writes=[B_bc])
        yield
        S.op("act", lambda e: e.activation(out=eb, in_=bc, func=AF.Exp), reads=[B_bc], writes=[B_eb])
        S.op("act", lambda e: e.activation(out=enb, in_=bc, func=AF.Exp, scale=-1.0), reads=[B_bc], writes=[B_enb])
        if not state_only:
            S.op("dve", lambda e: e.tensor_tensor(out=qe, in0=qsb, in1=eb, op=ALU.mult), reads=[B_q, B_eb], writes=[B_qe])
        else:
            c0_ = 63 if d == 0 else 0
            S.op("dve", lambda e, c0_=c0_: e.tensor_reduce(out=Lt1, in_=bc[:, :, c0_::64], axis=AX.X, op=ALU.add),
                 reads=[B_bc], writes=[B_exh])
            S.op("dve", lambda e: e.tensor_tensor(out=Ltot, in0=Ltot, in1=Lt1, op=ALU.add), reads=[B_exh], writes=[B_exh])
        S.op("dve", lambda e: e.scalar_tensor_tensor(out=nke, in0=fg, scalar=1.0, in1=enb, op0=ALU.subtract,
                                                     op1=ALU.mult), reads=[B_f, B_enb], writes=[B_nke])
        yield
        for pr in range(2):
            for h in range(4):
                S.op("pe", lambda e, pr=pr, h=h: e.transpose(out=ps7[:, h * 128:(h + 1) * 128],
                                                             in_=nke[:, h, pr * 128:(pr + 1) * 128], identity=ident),
                     reads=[B_nke, B_id], writes=[B_ps[7]])
            S.op("act", lambda e, pr=pr: e.activation(out=nkeT[:, pr, :, :],
                                                      in_=ps7[:, 0:512].rearrange("p (h n) -> p h n", h=4), func=AF.Copy),
                 reads=[B_ps[7]], writes=[B_nkeT[pr]])
        yield

    def chunks(seg, d, ti, state_only, tp, info, pull):
        eb, qe, nke, nkeT, vbf = eb2[tp], qe2[tp], nke2[tp], nkeT2[tp], vbf2[tp]
        B_eb, B_qe, B_nke, B_nkeT, B_v = B_eb2[tp], B_qe2[tp], B_nke2[tp], B_nkeT2[tp], B_v2[tp]
        xs = info["xs"]
        tok0 = info["tok0"]
        prs = (0, 1) if d == 0 else (1, 0)
        chs = (0, 1) if d == 0 else (1, 0)
        for pr in prs:
            pidx = ti * 2 + pr
            am = cnt["attm"] % 2
            cnt["attm"] += 1
            for h in (range(4) if not state_only else ()):
                S.op("pe", lambda e, pr=pr, h=h: e.matmul(patt[h], lhsT=nke[:, h, pr * 128:(pr + 1) * 128],
                                                          rhs=qe[:, h, pr * 128:(pr + 1) * 128], start=True, stop=True),
                     reads=[B_nke, B_qe], writes=[B_patt[h]])
            if not state_only:
                S.op("dve", lambda e, am=am, d=d: e.tensor_tensor(
                    out=attm[am], in0=PS[:, 4, :].rearrange("p (h n) -> p h n", h=4),
                    in1=negm[:, d:d + 1, :].to_broadcast([128, 4, 128]), op=ALU.mult),
                    reads=[B_ps[4], B_cst], writes=B_attm[am])
            pull(2)
            for ci, c in enumerate(chs):
                rows = slice(c * 64, (c + 1) * 64)
                t0 = pr * 128 + c * 64
                tcol = t0 + 63 if d == 0 else t0
                for h in (range(4) if not state_only else ()):
                    S.op("pe", lambda e, pr=pr, h=h, am=am, rows=rows, c=c: e.matmul(
                        pso[rows, h * 128:(h + 1) * 128], lhsT=attm[am][rows, h, c * 64:(c + 1) * 64],
                        rhs=vbf[rows, pr, h * 128:(h + 1) * 128], start=True, stop=False),
                        reads=[B_attm[am][h], B_v[pr]], writes=[B_ps[5]])
                    S.op("pe", lambda e, h=h, rows=rows, t0=t0: e.matmul(
                        pso[rows, h * 128:(h + 1) * 128], lhsT=qe[:, h, t0:t0 + 64], rhs=Sbf[:, h, :],
                        start=False, stop=True), reads=[B_qe, B_Sbf[h]], writes=[B_ps[5]])
                for h in range(4):
                    S.op("pe", lambda e, h=h, rows=rows, pr=pr: e.matmul(
                        psu[h], lhsT=nkeT[rows, pr, h, :], rhs=vbf[rows, pr, h * 128:(h + 1) * 128],
                        start=True, stop=True), reads=[B_nkeT[pr], B_v[pr]], writes=[B_psu[h]])
                S.op("dve", lambda e: e.tensor_tensor(out=Stmp, in0=Sst, in1=PS[:, 6, :].rearrange("p (h n) -> p h n", h=4),
                                                      op=ALU.subtract), reads=B_S + [B_ps[6]], writes=B_Stmp)
                S.op("pool", lambda e, tcol=tcol: e.tensor_tensor(
                    out=Sst, in0=Stmp, in1=eb[:, :, tcol:tcol + 1].to_broadcast([128, 4, 128]), op=ALU.mult),
                    reads=B_Stmp + [B_eb], writes=B_S)
                if not state_only:
                    S.op("act", lambda e: e.activation(out=Sbf, in_=Sst, func=AF.Copy), reads=B_S, writes=B_Sbf)
                pull(4)
            if state_only:
                continue
            if d == 0:
                S.op("act", lambda e, pidx=pidx: e.activation(out=of[:, pidx, :], in_=pso, func=AF.Copy),
                     reads=[B_ps[5]], writes=[B_of[pidx]])
                continue
            S.op("dve", lambda e, pidx=pidx: e.tensor_tensor(out=osum, in0=pso, in1=of[:, pidx, :], op=ALU.add),
                 reads=[B_ps[5], B_of[pidx]], writes=[B_osum])
            for h in range(4):
                S.op("act", lambda e, h=h: e.activation(out=junk[:, 0:128], in_=osum[:, h * 128:(h + 1) * 128],
                                                        func=AF.Square, accum_out=st2[:, h:h + 1]),
                     reads=[B_osum], writes=[B_junk, B_st2])
            rstd_op(B_st2, st2[:, 0:4], st2[:, 4:8], 1.0 / 128)
            S.op("dve", lambda e: e.tensor_tensor(
                out=osum.rearrange("p (h n) -> p h n", h=4), in0=osum.rearrange("p (h n) -> p h n", h=4),
                in1=st2[:, 4:8].rearrange("p (h o) -> p h o", o=1).to_broadcast([128, 4, 128]), op=ALU.mult),
                reads=[B_osum, B_st2], writes=[B_osum])
            S.op("dve", lambda e, pidx=pidx: e.tensor_tensor(out=ya, in0=osum, in1=Gst[:, pidx, :], op=ALU.mult),
                 reads=[B_osum, B_G[pidx]], writes=[B_ya])
            for kc in range(4):
                S.op("pe", lambda e, kc=kc: e.transpose(out=ps7[:, 512 + kc * 128:512 + (kc + 1) * 128],
                                                        in_=ya[:, kc * 128:(kc + 1) * 128], identity=ident),
                     reads=[B_ya, B_id], writes=[B_ps[7]])
            S.op("act", lambda e: e.activation(out=yaT, in_=ps7[:, 512:1024].rearrange("p (h n) -> p h n", h=4),
                                               func=AF.Copy), reads=[B_ps[7]], writes=[B_yaT])
            ys = cnt["yb"] % 2
            cnt["yb"] += 1
            tp0 = tok0 + pr * 128
            S.dma("sp", lambda e, ys=ys, tp0=tp0: e.dma_start(out=ybt[ys], in_=ybd[tp0:tp0 + 128, :]),
                  writes=[B_ybt[ys]])
            for kc in range(4):
                S.op("pe", lambda e, kc=kc, ys=ys: e.transpose(out=ps7[:, kc * 128:(kc + 1) * 128],
                                                               in_=ybt[ys][:, kc * 128:(kc + 1) * 128], identity=ident),
                     reads=[B_ybt[ys], B_id], writes=[B_ps[7]])
            S.op("act", lambda e, ys=ys: e.activation(out=ybTs[ys], in_=ps7[:, 0:512].rearrange("p (h n) -> p h n", h=4),
                                                      func=AF.Copy), reads=[B_ps[7]], writes=[B_ybTs[ys]])
            xo = cnt["x1o"] % 2
            cnt["x1o"] += 1
            for hf in range(2):
                pb = 3 if hf == 0 else 2
                for kc in range(8):
                    lhs = yaT[:, kc, :] if kc < 4 else ybTs[ys][:, kc - 4, :]
                    S.op("pe", lambda e, kc=kc, lhs=lhs, hf=hf, pb=pb: e.matmul(
                        PS[:, pb, :], lhsT=lhs, rhs=woutb[:, kc, hf * 512:(hf + 1) * 512],
                        start=(kc == 0), stop=(kc == 7)),
                        reads=[B_yaT if kc < 4 else B_ybTs[ys], B_wout[kc]], writes=[B_ps[pb]])
                S.op("dve", lambda e, hf=hf, pb=pb, xs=xs, pr=pr, xo=xo: e.tensor_tensor(
                    out=x1o[xo][:, hf * 512:(hf + 1) * 512], in0=PS[:, pb, :],
                    in1=xt[xs][:, pr, hf * 512:(hf + 1) * 512], op=ALU.add),
                    reads=[B_ps[pb], B_xt[xs]], writes=[B_x1o[xo]])
            S.dma("sp", lambda e, xo=xo, tp0=tp0: e.dma_start(out=x1d[tp0:tp0 + 128, :], in_=x1o[xo]),
                  reads=[B_x1o[xo]])

    def hpass(seg, d, state_only, init):
        if True:
            if init:
                S.op("dve", lambda e: e.tensor_copy(out=Sst, in_=Smine), reads=[B_exh], writes=B_S)
                S.op("pool", lambda e: e.tensor_copy(out=Sbf, in_=Smine), reads=[B_exh], writes=B_Sbf)
                S.barrier()
            else:
                S.op("dve", lambda e: e.memset(Sst, 0.0), writes=B_S)
                S.op("pool", lambda e: e.memset(Sbf, 0.0), writes=B_Sbf)
            if state_only:
                S.op("dve", lambda e: e.memset(Ltot, 0.0), writes=[B_exh])
            tiles = range(tps) if d == 0 else range(tps - 1, -1, -1)
            tl = list(tiles)
            infos = [dict() for _ in tl]
            tps_ = [(cnt["tile"] + i) % 2 for i in range(len(tl))]
            cnt["tile"] += len(tl)
            gens = [prologue(seg, d, tl[i], state_only, tps_[i], infos[i]) for i in range(len(tl))]
            for _ in gens[0]:
                pass
            for i in range(len(tl)):
                fill = gens[i + 1] if i + 1 < len(tl) else None

                def pull(n, fill=fill):
                    if fill is not None and getattr(C, "pipeline", True):
                        for _ in range(n):
                            next(fill, None)
                chunks(seg, d, tl[i], state_only, tps_[i], infos[i], pull)
                if fill is not None:
                    for _ in fill:
                        pass

    def state_in(d):
        S.op("dve", lambda e: e.memset(Sin, 0.0), reads=[B_exh], writes=[B_exh])
        S.op("dve", lambda e: e.memset(Smine, 0.0), reads=[B_exh], writes=[B_exh])
        order = range(0, NCORES - 1) if d == 0 else range(NCORES - 1, 0, -1)
        for i_, r in enumerate(order):
            rn = r + 1 if d == 0 else r - 1
            gs = i_ % 2
            for k_ in range(4):
                kk = 4 * d + k_
                S.dma("sp", lambda e, r=r, gs=gs, kk=kk, k_=k_: e.dma_start(
                    out=grb[gs][:, k_ * 129:(k_ + 1) * 129], in_=C.d[f"exh_dst{kk}"][r * 128:(r + 1) * 128, :]),
                    reads=[B_exhdst[kk]], writes=[B_grb[gs]])
            for h in range(4):
                S.op("dve", lambda e, h=h, gs=gs: e.scalar_tensor_tensor(
                    out=Sin[:, h, :], in0=Sin[:, h, :], scalar=grb[gs][:, 512 + h:513 + h],
                    in1=grb[gs][:, h * 128:(h + 1) * 128], op0=ALU.mult, op1=ALU.add),
                    reads=[B_exh, B_grb[gs]], writes=[B_exh])
            S.op("dve", lambda e, rn=rn: e.scalar_tensor_tensor(
                out=Smine.rearrange("p h n -> p (h n)"), in0=Sin.rearrange("p h n -> p (h n)"), scalar=ohs[:, rn:rn + 1],
                in1=Smine.rearrange("p h n -> p (h n)"), op0=ALU.mult, op1=ALU.add),
                reads=[B_exh], writes=[B_exh])

    sample = nseg - 1 if getattr(C, "exchange", False) else None
    if sample is not None:
        S.dma("sp", lambda e: e.dma_start(out=ohs, in_=C.d["oh"]), writes=[B_exh])
        for d in range(2):
            hpass(sample, d, True, False)
            S.barrier()
            S.op("act", lambda e, d=d: e.activation(out=EXH[:, d * 516:d * 516 + 512], in_=Sst.rearrange("p h n -> p (h n)"),
                                                    func=AF.Copy), reads=B_S, writes=[B_exh])
            S.op("act", lambda e, d=d: e.activation(out=EXH[:, d * 516 + 512:d * 516 + 516], in_=Ltot, func=AF.Exp),
                 reads=[B_exh], writes=[B_exh])
            for kk in range(4 * d, 4 * d + 4):
                S.dma("sp", lambda e, kk=kk: e.dma_start(out=C.d[f"exh_src{kk}"], in_=EXH[:, kk * 129:(kk + 1) * 129]),
                      reads=[B_exh], writes=[B_exhsrc[kk]])
                S.coll(lambda e, kk=kk: e.collective_compute("AllGather", ALU.bypass, replica_groups=[list(range(NCORES))],
                                                            ins=[C.d[f"exh_src{kk}"]], outs=[C.d[f"exh_dst{kk}"]]),
                       reads=[B_exhsrc[kk]], writes=[B_exhdst[kk]])
            S.barrier_local()
    for seg in range(nseg):
        for d in range(2):
            if seg == sample:
                S.barrier()
                state_in(d)
            hpass(seg, d, False, seg == sample)


TWO_PI = float(2 * np.pi)
NCT = 6


def _ctw(ct):
    return 96 if ct < 5 else 32


def phase_S(C, ntok):
    S, A, PS, nc = C.S, C.arena, C.psum, C.nc
    A.reset()
    W = 128
    B_ps = C.B_ps
    d = C.d

    winu = A.alloc([8, 512], BF16)
    gluw = A.alloc([4, 512], BF16)
    glub = A.alloc([512], BF16)
    onesr = A.alloc([128], BF16)
    mgrep = A.alloc([512], F32)
    g1c = A.alloc([8], F32)
    Ktab = A.alloc([NCT, 2 * 8, 32], BF16)
    magA = A.alloc([32], F32)
    dpw_r = A.alloc([8, 32], F32)
    dpw_i = A.alloc([8, 32], F32)
    Kv = Ktab.rearrange("p t (d u) c -> p t d u c", d=2)
    B_tab = S.buf("Stab")
    B_winu = S.buf("winu")
    mark_persist = A.off

    def T(shape, dt=F32):
        return A.alloc(shape, dt)
    Gtab = T([2, 16 * 9 * 2, 32], BF16)
    Gv = Gtab.rearrange("p d (a m r) c -> p d a m r c", a=16, m=9)
    lamr = T([32]); lami = T([32]); ldtall = T([64]); ldt = T([32]); dt_ = T([32]); rho = T([32]); th = T([32])
    ki = T([32], I32); t1 = T([32]); t2 = T([32]); t3 = T([32]); sn = T([32]); cs = T([32]); mag = T([32])
    pwr = T([9, 32]); pwi = T([9, 32]); cr = T([32]); ci = T([32])
    Br = T([32, 32]); Bi = T([32, 32]); Bbr = T([32, 32]); Bbi = T([32, 32])
    Bbr_bf = T([32, 32], BF16); Bbi_bf = T([32, 32], BF16)
    Craw = T([NCT, 2, 2, 64]); Lm = T([NCT, 2, 2, 128])
    Cp = T([2, 2, 16, 32])
    mh = T([2]); I32f = T([32]); I32b = T([32], BF16); dcol = T([NCT])
    X1 = T([32, 32]); X2 = T([32, 32]); X3 = T([32, 32]); X4 = T([32, 32])
    Kst = T([16, 256], BF16)
    Es = T([2, 32, W])
    m1st = [T([NCT, 2, 128], BF16) for _ in range(2)]
    stg = [T([512]) for _ in range(2)]
    _esf = Es.rearrange("p a l j -> p (a l j)")
    nL = NCT * 2 * 2 * 128
    LV = lambda ap: ap.rearrange("p (a b c e) -> p a b c e", a=NCT, b=2, c=2)
    Lhi = LV(_esf[:, 0:nL // 2].bitcast(BF16))
    Llo = LV(_esf[:, nL // 2:nL].bitcast(BF16))
    Lres = LV(_esf[:, nL:2 * nL])
    B_su = S.buf("Ssetup")
    B_stg = S.bufs("Sstg", 2)
    B_m1st = S.bufs("m1st", 2)
    RW = dict(reads=[B_su], writes=[B_su])

    def tt(out, a, b, op, eng="dve"):
        S.op(eng, lambda e: e.tensor_tensor(out=out, in0=a, in1=b, op=op), **RW)

    def ts(out, a, s1, s2=None, op0=ALU.mult, op1=None):
        if op1 is None:
            S.op("dve", lambda e: e.tensor_scalar(out=out, in0=a, scalar1=s1, scalar2=None, op0=op0), **RW)
        else:
            S.op("dve", lambda e: e.tensor_scalar(out=out, in0=a, scalar1=s1, scalar2=s2, op0=op0, op1=op1), **RW)

    def stt(out, a, sc, b, op0, op1):
        S.op("dve", lambda e: e.scalar_tensor_tensor(out=out, in0=a, scalar=sc, in1=b, op0=op0, op1=op1), **RW)

    def act(out, a, func, scale=None):
        if scale is None:
            S.op("act", lambda e: e.activation(out=out, in_=a, func=func), **RW)
        else:
            S.op("act", lambda e: e.activation(out=out, in_=a, func=func, scale=scale), **RW)

    def cp(out, a):
        S.op("dve", lambda e: e.tensor_copy(out=out, in_=a), **RW)

    def ms(out, v):
        S.op("dve", lambda e: e.memset(out, v), **RW)

    def cmul(or_, oi_, ar_, ai_, br_, bi_, ta, tb, tc):
        tt(ta, ar_, br_, ALU.mult)
        tt(tb, ai_, bi_, ALU.mult)
        tt(tc, ta, tb, ALU.subtract)
        tt(ta, ar_, bi_, ALU.mult)
        tt(tb, ai_, br_, ALU.mult)
        tt(oi_, ta, tb, ALU.add)
        cp(or_, tc)

    def dm(out, in_, slow=False):
        if slow:
            S.dma("sp", lambda e: e.dma_start(out=out, in_=in_, allow_slow_non_contiguous=True), writes=[B_su])
        else:
            S.dma("sp", lambda e: e.dma_start(out=out, in_=in_), writes=[B_su])

    for i_ in range(2):
        S.op("pool", lambda e, i_=i_: e.memset(m1st[i_], 0.0), writes=[B_m1st[i_]])
    ms(Br, 0.0)
    ms(Bi, 0.0)
    ms(Craw, 0.0)
    ms(dcol, 0.0)
    lv = lambda ap: ap.rearrange("p (d a) -> p d a", d=2)
    Bv = lambda ap: ap.rearrange("p (d a) c -> p d a c", d=2)
    for h in range(2):
        dm(lv(lamr)[h * 64:(h + 1) * 64], d["s5_lambda_re"].rearrange("d (a h) p -> h p d a", h=2)[h], slow=True)
        dm(lv(lami)[h * 64:(h + 1) * 64], d["s5_lambda_im"].rearrange("d (a h) p -> h p d a", h=2)[h], slow=True)
        dm(Bv(Br)[h * 64:(h + 1) * 64, :, :, h * 16:(h + 1) * 16],
           d["s5_b_re"].rearrange("d (a h) p c -> h p d a c", h=2)[h])
        dm(Bv(Bi)[h * 64:(h + 1) * 64, :, :, h * 16:(h + 1) * 16],
           d["s5_b_im"].rearrange("d (a h) p c -> h p d a c", h=2)[h])
    dm(ldtall, d["s5_log_dt"].rearrange("d g -> (d g)").partition_broadcast(128))
    for dr in range(2):
        for ri, nm in enumerate(("s5_c_re", "s5_c_im")):
            cflat = d[nm][dr].rearrange("g c p -> (g c) p")
            for ct in range(NCT):
                w_ = _ctw(ct)
                dm(Craw[0:w_, ct, dr, ri, :], cflat[96 * ct:96 * ct + w_, :])
    dm(mh, d["mh"])
    dm(I32f, d["i32"])
    for ct in range(NCT):
        w_ = _ctw(ct)
        dm(dcol[0:w_, ct:ct + 1], d["s5_d"][96 * ct:96 * ct + w_].rearrange("(q o) -> q o", o=1), slow=True)
    dm(g1c, d["norm1_g"].rearrange("(k p) -> p k", p=128), slow=True)
    dm(mgrep, d["s5_merge_g"].partition_broadcast(128))
    S.dma("sp", lambda e: e.dma_start(out=stg[0][0:1, 0:512], in_=d["s5_glu_b"].rearrange("(o c) -> o c", o=1)),
          writes=[B_stg[0]])
    S.op("dve", lambda e: e.tensor_copy(out=glub[0:1, :], in_=stg[0][0:1, 0:512]), reads=[B_stg[0]], writes=[B_tab])
    S.op("dve", lambda e: e.memset(onesr, 1.0), writes=[B_tab])
    S.op("dve", lambda e: e.memset(Ktab, 0.0), writes=[B_tab])
    cp(I32b, I32f)
    lda = ldtall.rearrange("p (d a h) -> p d a h", d=2, h=2)
    for h in range(2):
        cp(lv(ldt)[h * 64:(h + 1) * 64], lda[h * 64:(h + 1) * 64, :, :, h])
    n = 0
    for k in range(8):
        sl = n % 2
        S.dma("sp", lambda e, k=k, sl=sl: e.dma_start(out=stg[sl][:, 0:512], in_=d["w_in"][k * 128:(k + 1) * 128, 2560:3072]),
              writes=[B_stg[sl]])
        S.op("act", lambda e, k=k, sl=sl: e.activation(out=winu[:, k, :], in_=stg[sl][:, 0:512], func=AF.Copy,
                                                       scale=g1c[:, k:k + 1]), reads=[B_stg[sl], B_su], writes=[B_winu])
        n += 1
    for k in range(4):
        sl = n % 2
        S.dma("sp", lambda e, k=k, sl=sl: e.dma_start(out=stg[sl][:, 0:512], in_=d["s5_glu_w"][k * 128:(k + 1) * 128, :]),
              writes=[B_stg[sl]])
        S.op("act", lambda e, k=k, sl=sl: e.activation(out=gluw[:, k, :], in_=stg[sl][:, 0:512], func=AF.Copy),
             reads=[B_stg[sl]], writes=[B_winu])
        n += 1
    if getattr(C, "dbg_su", 99) == 1:
        S.barrier()
        return
    act(dt_, ldt, AF.Exp)
    tt(rho, lamr, dt_, ALU.mult)
    tt(th, lami, dt_, ALU.mult)
    ts(ki, th, 1.0 / TWO_PI)
    cp(t1, ki)
    stt(t2, t1, -TWO_PI, th, ALU.mult, ALU.add)
    ts(t1, t2, float(np.pi), TWO_PI, ALU.is_gt, ALU.mult)
    tt(t2, t2, t1, ALU.subtract)
    ts(t1, t2, float(-np.pi), TWO_PI, ALU.is_lt, ALU.mult)
    tt(t2, t2, t1, ALU.add)
    act(sn, t2, AF.Sin)
    ts(t3, t2, float(np.pi / 2), None, ALU.add)
    ts(t1, t3, float(np.pi), TWO_PI, ALU.is_gt, ALU.mult)
    tt(t3, t3, t1, ALU.subtract)
    act(cs, t3, AF.Sin)
    ts(t1, rho, 1.0 / 5, 1.0, ALU.mult, ALU.add)
    for dv in (4.0, 3.0, 2.0):
        tt(t1, t1, rho, ALU.mult)
        ts(t1, t1, 1.0 / dv, 1.0, ALU.mult, ALU.add)
    tt(t1, t1, rho, ALU.mult)
    ts(mag, t1, 1.0, None, ALU.add)
    ms(pwr[:, 0, :], 1.0)
    ms(pwi[:, 0, :], 0.0)
    tt(pwr[:, 1, :], mag, cs, ALU.mult)
    tt(pwi[:, 1, :], mag, sn, ALU.mult)
    for m in range(2, 9):
        cmul(pwr[:, m, :], pwi[:, m, :], pwr[:, m - 1, :], pwi[:, m - 1, :], pwr[:, 1, :], pwi[:, 1, :], t1, t2, t3)
    ts(t1, pwr[:, 1, :], -1.0, None, ALU.add)
    tt(t2, lamr, lamr, ALU.mult)
    tt(t3, lami, lami, ALU.mult)
    tt(t2, t2, t3, ALU.add)
    S.op("dve", lambda e: e.reciprocal(out=t2, in_=t2), **RW)
    tt(cr, t1, lamr, ALU.mult)
    tt(t3, pwi[:, 1, :], lami, ALU.mult)
    tt(cr, cr, t3, ALU.add)
    tt(cr, cr, t2, ALU.mult)
    tt(ci, pwi[:, 1, :], lamr, ALU.mult)
    tt(t3, t1, lami, ALU.mult)
    tt(ci, ci, t3, ALU.subtract)
    tt(ci, ci, t2, ALU.mult)
    bc32 = lambda ap: ap.rearrange("p (l o) -> p l o", o=1).to_broadcast([128, 32, 32])
    cmul(Bbr, Bbi, bc32(cr), bc32(ci), Br, Bi, X1, X2, X3)
    cp(Bbr_bf, Bbr)
    cp(Bbi_bf, Bbi)
    if getattr(C, "dbg_su", 99) == 2:
        S.barrier()
        return
    for hh in range(2):
        S.op("dve", lambda e, hh=hh: e.tensor_scalar(out=Lm[:, :, :, :, hh * 64:(hh + 1) * 64], in0=Craw,
                                                     scalar1=mh[:, hh:hh + 1], scalar2=None, op0=ALU.mult), **RW)
    cp(Lhi, Lm)
    tt(Lres, Lm, Lhi, ALU.subtract)
    cp(Llo, Lres)
    for dr in range(2):
        for ri in range(2):
            for a3 in range(3):
                pb = a3
                nn = len(range(a3, 16, 3))
                for ct in range(nn):
                    rows = slice(32 * a3, 32 * a3 + 32)
                    for li, Lx in enumerate((Lhi, Llo)):
                        S.op("pe", lambda e, pb=pb, ct=ct, rows=rows, dr=dr, ri=ri, Lx=Lx, li=li: e.matmul(
                            PS[:, pb, ct * 32:(ct + 1) * 32], lhsT=Lx[rows, ct, dr, ri, :], rhs=I32b[rows, :],
                            start=(li == 0), stop=(li == 1)), reads=[B_su], writes=[B_ps[pb]])
                S.op("act", lambda e, pb=pb, dr=dr, ri=ri, a3=a3, nn=nn: e.activation(
                    out=Cp[:, ri, dr, a3::3, :], in_=PS[:, pb, 0:32 * nn].rearrange("p (a c) -> p a c", a=nn), func=AF.Copy),
                    reads=[B_ps[pb]], writes=[B_su])
    if getattr(C, "dbg_su", 99) == 3:
        S.barrier()
        return
    tt(t1, mag, mag, ALU.mult)
    tt(t1, t1, t1, ALU.mult)
    tt(magA, t1, t1, ALU.mult)
    S.op("dve", lambda e: e.reciprocal(out=t2, in_=magA), **RW)
    tt(dpw_r[:, 0, :], pwr[:, 8, :], t2, ALU.mult)
    tt(dpw_i[:, 0, :], pwi[:, 8, :], t2, ALU.mult)
    for k in range(1, 8):
        cmul(dpw_r[:, k, :], dpw_i[:, k, :], dpw_r[:, k - 1, :], dpw_i[:, k - 1, :], dpw_r[:, k - 1, :], dpw_i[:, k - 1, :],
             t1, t2, t3)
    ms(Es[:, 0, :, 0:1], 1.0)
    ms(Es[:, 1, :, 0:1], 0.0)
    for k in range(7):
        w = 1 << k
        bcw = lambda ap, w=w: ap.rearrange("p (l o) -> p l o", o=1).to_broadcast([128, 32, w])
        Y1 = Lm.rearrange("p a b c e -> p (a b c e)")[:, 0:32 * w].rearrange("p (l j) -> p l j", l=32)
        Y2 = Lm.rearrange("p a b c e -> p (a b c e)")[:, 2048:2048 + 32 * w].rearrange("p (l j) -> p l j", l=32) \
            if w <= 32 else C_tmpbig(A, Br, Bi, w)
        Y3 = X1.rearrange("p a c -> p (a c)")[:, 0:32 * w].rearrange("p (l j) -> p l j", l=32) if w <= 32 \
            else C_tmpbig(A, X1, X2, w)
        cmul(Es[:, 0, :, w:2 * w], Es[:, 1, :, w:2 * w], Es[:, 0, :, 0:w], Es[:, 1, :, 0:w],
             bcw(dpw_r[:, k, :]), bcw(dpw_i[:, k, :]), Y1, Y2, Y3)
    if getattr(C, "dbg_su", 99) == 4:
        S.barrier()
        return
    Cr_ = Cp[:, 0, :, :, :].rearrange("p d a c -> p (d a) c")
    Ci_ = Cp[:, 1, :, :, :].rearrange("p d a c -> p (d a) c")
    X1v = lambda X, dr: X.rearrange("p (d a) c -> p d a c", d=2)[:, dr]
    for m in range(9):
        ar_b, ai_b = bc32(pwr[:, m, :]), bc32(pwi[:, m, :])
        tt(X1, Cr_, ar_b, ALU.mult)
        tt(X2, Ci_, ai_b, ALU.mult)
        tt(X3, X1, X2, ALU.subtract)
        tt(X1, Cr_, ai_b, ALU.mult)
        tt(X2, Ci_, ar_b, ALU.mult)
        stt(X4, X1, -1.0, X2, ALU.mult, ALU.subtract)
        for dr in range(2):
            ms_ = m if dr == 0 else 8 - m
            cp(Gv[:, dr, :, ms_, 0, :], X1v(X3, dr))
            cp(Gv[:, dr, :, ms_, 1, :], X1v(X4, dr))
    if getattr(C, "dbg_su", 99) == 5:
        S.barrier()
        return
    for dr in range(2):
        s0 = 0 if dr == 0 else 1
        for a in range(16):
            pb = 4 + (a % 2)
            for ri, Bx in enumerate((Bbr_bf, Bbi_bf)):
                S.op("pe", lambda e, pb=pb, a=a, dr=dr, ri=ri, Bx=Bx, s0=s0: e.matmul(
                    PS[0:32, pb, 0:256], lhsT=Bx[:, dr * 16 + a, :], rhs=Gv[:, dr, a, s0:s0 + 8, ri, :],
                    start=(ri == 0), stop=(ri == 1)), reads=[B_su], writes=[B_ps[pb]])
            S.op("act", lambda e, pb=pb, a=a: e.activation(out=Kst[0:32, a, :], in_=PS[0:32, pb, 0:256],
                                                          func=AF.Copy), reads=[B_ps[pb]], writes=[B_su])
        Kstv = Kst.rearrange("p a (u c) -> p a u c", u=8)
        for a3 in range(3):
            nn = len(range(a3, 16, 3))
            S.dma("sp", lambda e, a3=a3, dr=dr, nn=nn, Kstv=Kstv: e.dma_start(
                out=Kv[32 * a3:32 * a3 + 32, 0:nn, dr, :, :], in_=Kstv[0:32, a3::3, :, :]),
                reads=[B_su], writes=[B_tab])
    for ct in range(NCT):
        S.op("dve", lambda e, ct=ct: e.scalar_tensor_tensor(out=Kv[:, ct, 0, 0, :], in0=I32f, scalar=dcol[:, ct:ct + 1],
                                                            in1=Kv[:, ct, 0, 0, :], op0=ALU.mult, op1=ALU.add),
             reads=[B_su, B_tab], writes=[B_tab])
    if getattr(C, "dbg_su", 99) == 6:
        S.barrier()
        return
    m1d = d["m1_scr"].rearrange("p (t d s r c) -> p t d s r c", t=NCT, d=2, s=8, r=2)
    nst = 0
    for n_ in range(8):
        cmul(X1, X2, bc32(pwr[:, n_, :]), bc32(pwi[:, n_, :]), Bbr, Bbi, X3, X4, X1)
        for dr in range(2):
            sp = (7 - n_) if dr == 0 else n_
            si_ = nst % 2
            nst += 1
            for ct in range(NCT):
                w_ = _ctw(ct)
                np_ = w_ // 32
                l0 = dr * 16 + ct * 3
                for ri, Xq in enumerate((X1, X2)):
                    pb = 6 + ri
                    S.op("pe", lambda e, pb=pb, l0=l0, np_=np_, w_=w_, Xq=Xq: e.transpose(
                        out=PS[0:w_, pb, 0:128], in_=Xq[:, l0:l0 + np_, :].rearrange("p a c -> p (a c)"),
                        identity=C.ident_f), reads=[B_su, C.B_identf], writes=[B_ps[pb]])
                    S.op("act", lambda e, pb=pb, w_=w_, ct=ct, ri=ri, si_=si_: e.activation(
                        out=m1st[si_][0:w_, ct, ri, :], in_=PS[0:w_, pb, 0:128], func=AF.Copy),
                        reads=[B_ps[pb]], writes=[B_m1st[si_]])
            S.dma("sp", lambda e, si_=si_, dr=dr, sp=sp: e.dma_start(out=m1d[:, :, dr, sp, :, :], in_=m1st[si_]),
                  reads=[B_m1st[si_]])
    if getattr(C, "dbg_su", 99) == 7:
        S.barrier()
        return
    S.dma("sp", lambda e: e.dma_start(out=d["e_scr"], in_=Es.rearrange("p a l j -> p (a l j)")), reads=[B_su])
    S.dma("sp", lambda e: e.dma_start(out=d["g_scr"], in_=Gtab.rearrange("p d k c -> p (d k c)")), reads=[B_su])
    S.barrier()
    A.reset(mark_persist)
    if getattr(C, "dbg_scut", 9) <= 0:
        return
    phase_S_main(C, ntok, dict(winu=winu, gluw=gluw, glub=glub, onesr=onesr, mgrep=mgrep, Kv=Kv, magA=magA,
                               dpw_r=dpw_r, dpw_i=dpw_i, B_tab=B_tab, B_winu=B_winu))


def C_tmpbig(A, t_a, t_b, w):
    fa = t_a.rearrange("p a c -> p (a c)")
    fb = t_b.rearrange("p a c -> p (a c)")
    assert fb.offset == fa.offset + 1024, (fa.offset, fb.offset)
    return A.t[:, fa.offset:fa.offset + 32 * w].rearrange("p (l j) -> p l j", l=32)


def phase_S_main(C, ntok, Tb):
    S, A, PS, nc = C.S, C.arena, C.psum, C.nc
    seglen = C.seglen
    nseg = ntok // seglen
    NJ = seglen // 8
    W = 128
    nwin = NJ // W
    nblk = NJ // 128
    TT = 256
    ident = C.ident_bf
    B_id = C.B_ident
    B_ps = C.B_ps
    d = C.d
    winu, gluw, glub, onesr, mgrep = Tb["winu"], Tb["gluw"], Tb["glub"], Tb["onesr"], Tb["mgrep"]
    Kv, magA, dpw_r, dpw_i = Tb["Kv"], Tb["magA"], Tb["dpw_r"], Tb["dpw_i"]
    B_tab, B_winu = Tb["B_tab"], Tb["B_winu"]

    uT = A.alloc([NCT, seglen], BF16)
    Xp = [[A.alloc([16, NJ + 1], BF16) for _ in range(2)] for _ in range(2)]
    B_uT = S.bufs("uT", NCT)
    B_Xp = [S.bufs(f"Xp{dr}_", 2) for dr in range(2)]
    mark = A.off
    xt = [A.alloc([2, D], F32) for _ in range(2)]
    hbf = A.alloc([D], BF16)
    hT = A.alloc([8, TT], BF16)
    junk = A.alloc([D], BF16)
    st1 = A.alloc([4], F32)
    B_xt = S.bufs("Sxt", 2)
    B_hbf = S.buf("Shbf")
    B_hT = S.buf("ShT")
    B_junk = S.buf("Sjunk")
    B_st1 = S.buf("Sst1")
    endA = A.off
    A.reset(mark)
    M1 = A.alloc([NCT, 32, 128], BF16)
    Ev = A.alloc([2, 32, W], F32)
    M1v = M1.rearrange("p t (d s r) c -> p t d s r c", d=2, s=8)
    smr = [A.alloc([NJ], F32) for _ in range(2)]
    smi = [A.alloc([NJ], F32) for _ in range(2)]
    Yr = [A.alloc([NJ], F32) for _ in range(2)]
    Yi = [A.alloc([NJ], F32) for _ in range(2)]
    q1 = A.alloc([NJ], F32); q2 = A.alloc([NJ], F32); q3 = A.alloc([NJ], F32); q4 = A.alloc([NJ], F32)
    r1 = A.alloc([NJ], F32); r2 = A.alloc([NJ], F32); r3 = A.alloc([NJ], F32); r4 = A.alloc([NJ], F32)
    cry = A.alloc([8], F32)
    B_M1 = S.buf("M1")
    B_E = S.buf("Etab")
    B_sm = S.bufs("sm", 2)
    B_Y = S.bufs("Y", 2)
    B_q = S.buf("q")
    B_r = S.buf("r")
    B_cry = S.buf("cry")
    endB = A.off
    A.reset(mark)
    Gtab = A.alloc([2, 16 * 9 * 2, 32], BF16)
    Gv = Gtab.rearrange("p d (a m r) c -> p d a m r c", a=16, m=9)
    zf = A.alloc([8, 512], F32)
    zbf = A.alloc([8, 512], BF16)
    zT = [A.alloc([4, 128], BF16) for _ in range(2)]
    gate = [A.alloc([512], F32) for _ in range(2)]
    ybo = [A.alloc([8, 512], BF16) for _ in range(2)]
    junk2 = A.alloc([512], BF16)
    st2 = A.alloc([4], F32)
    B_G = S.buf("Gtab")
    B_zf = S.buf("zf")
    B_zbf = S.buf("zbf")
    B_zT = S.bufs("zT", 2)
    B_gate = S.bufs("gate", 2)
    B_ybo = S.bufs("ybo", 2)
    B_junk2 = S.buf("junk2")
    B_st2 = S.buf("st2")
    A.reset(max(A.off, endA, endB))

    xd = d["x"]
    ybd = d["yb"]
    cnt = {"x": 0, "pin": 0, "s": 0, "y": 0, "z": 0, "o": 0}

    for dr in range(2):
        for ri in range(2):
            col = 0 if dr == 0 else NJ
            S.op("pool", lambda e, dr=dr, ri=ri, col=col: e.memset(Xp[dr][ri][:, :, col:col + 1], 0.0),
                 writes=[B_Xp[dr][ri]])

    def rstd_op(B, ss_ap, out_ap, inv_n):
        S.op("dve", lambda e: e.tensor_scalar(out=out_ap, in0=ss_ap, scalar1=inv_n, scalar2=EPS,
                                              op0=ALU.mult, op1=ALU.add), reads=[B], writes=[B])
        S.op("act", lambda e: e.activation(out=out_ap, in_=out_ap, func=AF.Sqrt), reads=[B], writes=[B])
        S.op("dve", lambda e: e.reciprocal(out=out_ap, in_=out_ap), reads=[B], writes=[B])

    Eloc = A.alloc([2, 2, 16], F32)
    G8 = A.alloc([NCORES, 64], F32)
    CH = A.alloc([NCORES, 2, 32], F32)
    Cm = A.alloc([2, 32], F32)
    Yin = A.alloc([2, 32], F32)
    a256 = A.alloc([2, 32], F32)
    ohs = A.alloc([NCORES], F32)
    e1 = A.alloc([32], F32); e2 = A.alloc([32], F32); e3 = A.alloc([32], F32)
    B_ex = S.buf("Sex")
    B_exsrc = S.buf("exs_src")
    B_exdst = S.buf("exs_dst")
    EX = dict(reads=[B_ex, B_tab], writes=[B_ex])

    def stageA(seg):
        for ti in range(seglen // TT):
            tok0 = seg * seglen + ti * TT
            xs = cnt["x"] % 2
            cnt["x"] += 1
            S.dma("sp", lambda e, tok0=tok0, xs=xs: e.dma_start(
                out=xt[xs], in_=xd[tok0:tok0 + TT, :].rearrange("(s p) f -> p s f", p=128)), writes=[B_xt[xs]])
            for sub in range(2):
                S.op("act", lambda e, xs=xs, sub=sub: e.activation(out=junk, in_=xt[xs][:, sub, :], func=AF.Square,
                                                                 accum_out=st1[:, 0:1]),
                     reads=[B_xt[xs]], writes=[B_junk, B_st1])
                rstd_op(B_st1, st1[:, 0:1], st1[:, 1:2], 1.0 / D)
                S.op("dve", lambda e, xs=xs, sub=sub: e.tensor_scalar(out=hbf, in0=xt[xs][:, sub, :],
                                                                    scalar1=st1[:, 1:2], scalar2=None, op0=ALU.mult),
                     reads=[B_xt[xs], B_st1], writes=[B_hbf])
                ps7 = PS[:, 7, :].bitcast(BF16)
                for k in range(8):
                    S.op("pe", lambda e, k=k, ps7=ps7: e.transpose(out=ps7[:, k * 128:(k + 1) * 128],
                                                                   in_=hbf[:, k * 128:(k + 1) * 128], identity=ident),
                         reads=[B_hbf, B_id], writes=[B_ps[7]])
                S.op("act", lambda e, sub=sub, ps7=ps7: e.activation(
                    out=hT[:, :, sub * 128:(sub + 1) * 128], in_=ps7.rearrange("p (k n) -> p k n", k=8), func=AF.Copy),
                    reads=[B_ps[7]], writes=[B_hT])
            for ct in range(NCT):
                w_ = _ctw(ct)
                pb = cnt["pin"] % 2
                cnt["pin"] += 1
                for k in range(8):
                    S.op("pe", lambda e, k=k, ct=ct, pb=pb, w_=w_: e.matmul(
                        PS[0:w_, pb, 0:TT], lhsT=winu[:, k, ct * 96:ct * 96 + w_], rhs=hT[:, k, :],
                        start=(k == 0), stop=(k == 7)), reads=[B_winu, B_hT], writes=[B_ps[pb]])
                S.op("act", lambda e, ct=ct, pb=pb, ti=ti, w_=w_: e.activation(
                    out=uT[0:w_, ct, ti * TT:(ti + 1) * TT], in_=PS[0:w_, pb, 0:TT], func=AF.Copy),
                    reads=[B_ps[pb]], writes=[B_uT[ct]])

    def stageB(seg, carry, extract):
        S.dma("sp", lambda e: e.dma_start(out=M1.rearrange("p t k c -> p (t k c)"), in_=d["m1_scr"]), writes=[B_M1])
        S.dma("sp", lambda e: e.dma_start(out=Ev.rearrange("p a l j -> p (a l j)"), in_=d["e_scr"]), writes=[B_E])
        for dr in range(2):
            for a in range(16):
                ct, a3 = a // 3, a % 3
                rows = slice(32 * a3, 32 * a3 + 32)
                lane = dr * 16 + a
                sb = cnt["s"] % 2
                cnt["s"] += 1
                pb = 2 + a3
                for ri in range(2):
                    for sp in range(8):
                        S.op("pe", lambda e, pb=pb, ri=ri, sp=sp, rows=rows, ct=ct, dr=dr: e.matmul(
                            PS[:, pb, ri * NJ:(ri + 1) * NJ], lhsT=M1v[rows, ct, dr, sp, ri, :],
                            rhs=uT[rows, ct, sp::8], start=(sp == 0), stop=(sp == 7)),
                            reads=[B_M1, B_uT[ct]], writes=[B_ps[pb]])
                sr = PS[:, pb, 0:NJ].rearrange("p (w j) -> p w j", w=nwin)
                si = PS[:, pb, NJ:2 * NJ].rearrange("p (w j) -> p w j", w=nwin)
                w3 = lambda ap: ap.rearrange("p (w j) -> p w j", w=nwin)
                if dr == 0:
                    Ec = Ev[:, 0, lane, :]
                    Es_ = Ev[:, 1, lane, :]
                else:
                    Ec = Ev[:, 0, lane, ::-1]
                    Es_ = Ev[:, 1, lane, ::-1]
                Ecb = Ec.rearrange("p (o j) -> p o j", o=1).to_broadcast([128, nwin, W])
                Esb = Es_.rearrange("p (o j) -> p o j", o=1).to_broadcast([128, nwin, W])
                S.op("dve", lambda e, sr=sr, Ecb=Ecb: e.tensor_tensor(out=w3(q1), in0=sr, in1=Ecb, op=ALU.mult),
                     reads=[B_ps[pb], B_E], writes=[B_q])
                S.op("dve", lambda e, si=si, Esb=Esb: e.tensor_tensor(out=w3(q2), in0=si, in1=Esb, op=ALU.mult),
                     reads=[B_ps[pb], B_E], writes=[B_q])
                S.op("dve", lambda e, sb=sb: e.tensor_tensor(out=smr[sb], in0=q1, in1=q2, op=ALU.add),
                     reads=[B_q], writes=[B_sm[sb]])
                S.op("dve", lambda e, si=si, Ecb=Ecb: e.tensor_tensor(out=w3(q3), in0=si, in1=Ecb, op=ALU.mult),
                     reads=[B_ps[pb], B_E], writes=[B_q])
                S.op("dve", lambda e, sr=sr, Esb=Esb: e.tensor_tensor(out=w3(q4), in0=sr, in1=Esb, op=ALU.mult),
                     reads=[B_ps[pb], B_E], writes=[B_q])
                S.op("dve", lambda e, sb=sb: e.tensor_tensor(out=smi[sb], in0=q3, in1=q4, op=ALU.subtract),
                     reads=[B_q], writes=[B_sm[sb]])
                mgb = magA[:, lane:lane + 1].to_broadcast([128, W])
                for wi in range(nwin):
                    if dr == 0:
                        sl = slice(wi * W, (wi + 1) * W)
                        vw = lambda ap, sl=sl: ap[:, sl]
                        lastc = (wi + 1) * W - 1
                    else:
                        lo = NJ - (wi + 1) * W
                        sl = slice(lo, lo + W)
                        vw = lambda ap, sl=sl: ap[:, sl][:, ::-1]
                        lastc = lo
                    if wi == 0:
                        ini_r = Yin[:, 0, lane:lane + 1] if carry else 0.0
                        ini_i = Yin[:, 1, lane:lane + 1] if carry else 0.0
                    else:
                        ini_r = cry[:, 0:1]
                        ini_i = cry[:, 1:2]
                    S.op("dve", lambda e, sb=sb, vw=vw, ini_r=ini_r, mgb=mgb: e.tensor_tensor_scan(
                        out=vw(Yr[sb]), data0=mgb, data1=vw(smr[sb]), initial=ini_r, op0=ALU.mult, op1=ALU.add),
                        reads=[B_sm[sb], B_tab, B_cry, B_ex], writes=[B_Y[sb]])
                    S.op("dve", lambda e, sb=sb, vw=vw, ini_i=ini_i, mgb=mgb: e.tensor_tensor_scan(
                        out=vw(Yi[sb]), data0=mgb, data1=vw(smi[sb]), initial=ini_i, op0=ALU.mult, op1=ALU.add),
                        reads=[B_sm[sb], B_tab, B_cry, B_ex], writes=[B_Y[sb]])
                    if wi < nwin - 1:
                        Rr = dpw_r[:, 7, lane:lane + 1]
                        Ri = dpw_i[:, 7, lane:lane + 1]
                        yl_r = Yr[sb][:, lastc:lastc + 1]
                        yl_i = Yi[sb][:, lastc:lastc + 1]
                        S.op("dve", lambda e, yl_i=yl_i, Ri=Ri: e.tensor_scalar(out=cry[:, 2:3], in0=yl_i, scalar1=Ri, scalar2=None,
                                                                               op0=ALU.mult), reads=[B_Y[sb], B_tab], writes=[B_cry])
                        S.op("dve", lambda e, yl_r=yl_r, Rr=Rr: e.scalar_tensor_tensor(out=cry[:, 0:1], in0=yl_r, scalar=Rr,
                                                                                      in1=cry[:, 2:3], op0=ALU.mult, op1=ALU.subtract),
                             reads=[B_Y[sb], B_tab, B_cry], writes=[B_cry])
                        S.op("dve", lambda e, yl_i=yl_i, Rr=Rr: e.tensor_scalar(out=cry[:, 3:4], in0=yl_i, scalar1=Rr, scalar2=None,
                                                                               op0=ALU.mult), reads=[B_Y[sb], B_tab], writes=[B_cry])
                        S.op("dve", lambda e, yl_r=yl_r, Ri=Ri: e.scalar_tensor_tensor(out=cry[:, 1:2], in0=yl_r, scalar=Ri,
                                                                                      in1=cry[:, 3:4], op0=ALU.mult, op1=ALU.add),
                             reads=[B_Y[sb], B_tab, B_cry], writes=[B_cry])
                if extract:
                    ecl = Ev[:, 0, lane, W - 1:W]
                    esl = Ev[:, 1, lane, W - 1:W]
                    ylr = Yr[sb][:, lastc:lastc + 1]
                    yli = Yi[sb][:, lastc:lastc + 1]
                    S.op("dve", lambda e, yli=yli, esl=esl: e.tensor_scalar(out=cry[:, 4:5], in0=yli, scalar1=esl, scalar2=None,
                                                                           op0=ALU.mult), reads=[B_Y[sb], B_E], writes=[B_cry])
                    S.op("dve", lambda e, ylr=ylr, ecl=ecl, dr=dr, a=a: e.scalar_tensor_tensor(
                        out=Eloc[:, dr, 0, a:a + 1], in0=ylr, scalar=ecl, in1=cry[:, 4:5], op0=ALU.mult, op1=ALU.subtract),
                        reads=[B_Y[sb], B_E, B_cry], writes=[B_ex])
                    S.op("dve", lambda e, yli=yli, ecl=ecl: e.tensor_scalar(out=cry[:, 5:6], in0=yli, scalar1=ecl, scalar2=None,
                                                                           op0=ALU.mult), reads=[B_Y[sb], B_E], writes=[B_cry])
                    S.op("dve", lambda e, ylr=ylr, esl=esl, dr=dr, a=a: e.scalar_tensor_tensor(
                        out=Eloc[:, dr, 1, a:a + 1], in0=ylr, scalar=esl, in1=cry[:, 5:6], op0=ALU.mult, op1=ALU.add),
                        reads=[B_Y[sb], B_E, B_cry], writes=[B_ex])
                oc = slice(1, NJ + 1) if dr == 0 else slice(0, NJ)
                S.op("pool", lambda e, sb=sb, Ecb=Ecb: e.tensor_tensor(out=w3(r1), in0=w3(Yr[sb]), in1=Ecb, op=ALU.mult),
                     reads=[B_Y[sb], B_E], writes=[B_r])
                S.op("pool", lambda e, sb=sb, Esb=Esb: e.tensor_tensor(out=w3(r2), in0=w3(Yi[sb]), in1=Esb, op=ALU.mult),
                     reads=[B_Y[sb], B_E], writes=[B_r])
                S.op("pool", lambda e, dr=dr, a=a, oc=oc: e.tensor_tensor(out=Xp[dr][0][:, a, oc], in0=r1, in1=r2, op=ALU.subtract),
                     reads=[B_r], writes=[B_Xp[dr][0]])
                S.op("pool", lambda e, sb=sb, Esb=Esb: e.tensor_tensor(out=w3(r3), in0=w3(Yr[sb]), in1=Esb, op=ALU.mult),
                     reads=[B_Y[sb], B_E], writes=[B_r])
                S.op("pool", lambda e, sb=sb, Ecb=Ecb: e.tensor_tensor(out=w3(r4), in0=w3(Yi[sb]), in1=Ecb, op=ALU.mult),
                     reads=[B_Y[sb], B_E], writes=[B_r])
                S.op("pool", lambda e, dr=dr, a=a, oc=oc: e.tensor_tensor(out=Xp[dr][1][:, a, oc], in0=r3, in1=r4, op=ALU.add),
                     reads=[B_r], writes=[B_Xp[dr][1]])

    def stageC(seg):
        S.dma("sp", lambda e: e.dma_start(out=Gtab.rearrange("p d k c -> p (d k c)"), in_=d["g_scr"]), writes=[B_G])
        for jb in range(nblk):
            j0 = jb * 128
            for a in range(16):
                ct, a3 = a // 3, a % 3
                rows = slice(32 * a3, 32 * a3 + 32)
                pb = a3
                PSy = PS[:, pb, 0:256]
                first = True
                for dr in range(2):
                    c0 = j0 if dr == 0 else j0 + 1
                    s0 = 1 if dr == 0 else 0
                    for ri in range(2):
                        S.op("pe", lambda e, PSy=PSy, dr=dr, ri=ri, a=a, c0=c0, s0=s0, first=first: e.matmul(
                            PSy, lhsT=Xp[dr][ri][:, a, c0:c0 + 128], rhs=Gv[:, dr, a, s0:s0 + 8, ri, :],
                            start=first, stop=False), reads=[B_Xp[dr][ri], B_G], writes=[B_ps[pb]])
                        first = False
                for dr in range(2):
                    for sp in range(8):
                        t0 = 8 * j0 + sp
                        lhs = uT[rows, ct, t0:t0 + 8 * 127 + 1:8]
                        if dr == 0:
                            rhs = Kv[rows, ct, 0, 0:8 - sp, :]
                            out = PSy[:, 32 * sp:256]
                        else:
                            rhs = Kv[rows, ct, 1, 7 - sp:8, :]
                            out = PSy[:, 0:32 * (sp + 1)]
                        S.op("pe", lambda e, out=out, lhs=lhs, rhs=rhs, last=(dr == 1 and sp == 7): e.matmul(
                            out, lhsT=lhs, rhs=rhs, start=False, stop=last),
                            reads=[B_uT[ct], B_tab], writes=[B_ps[pb]])
                S.op("act", lambda e, PSy=PSy, a=a: e.activation(out=zf[:, :, a * 32:(a + 1) * 32],
                                                                in_=PSy.rearrange("p (t c) -> p t c", t=8),
                                                                func=AF.Gelu_apprx_tanh), reads=[B_ps[pb]], writes=[B_zf])
            if getattr(C, "dbg_ccut", 9) <= 1:
                continue
            S.op("pool", lambda e: e.tensor_copy(out=zbf, in_=zf), reads=[B_zf], writes=[B_zbf])
            yo = cnt["o"] % 2
            cnt["o"] += 1
            for tp in range(8):
                zi = cnt["z"] % 2
                cnt["z"] += 1
                ps6 = PS[:, 6, :].bitcast(BF16)
                for kc in range(4):
                    S.op("pe", lambda e, tp=tp, kc=kc, ps6=ps6: e.transpose(out=ps6[:, kc * 128:(kc + 1) * 128],
                                                                            in_=zbf[:, tp, kc * 128:(kc + 1) * 128], identity=ident),
                         reads=[B_zbf, B_id], writes=[B_ps[6]])
                S.op("act", lambda e, zi=zi, ps6=ps6: e.activation(out=zT[zi], in_=ps6[:, 0:512].rearrange("p (k n) -> p k n", k=4),
                                                                  func=AF.Copy), reads=[B_ps[6]], writes=[B_zT[zi]])
                for kc in range(4):
                    S.op("pe", lambda e, kc=kc, zi=zi: e.matmul(PS[:, 7, :], lhsT=zT[zi][:, kc, :], rhs=gluw[:, kc, :],
                                                                start=(kc == 0), stop=False),
                         reads=[B_zT[zi], B_winu], writes=[B_ps[7]])
                S.op("pe", lambda e: e.matmul(PS[:, 7, :], lhsT=onesr[0:1, :], rhs=glub[0:1, :], start=False, stop=True),
                     reads=[B_tab], writes=[B_ps[7]])
                S.op("act", lambda e, zi=zi: e.activation(out=gate[zi], in_=PS[:, 7, :], func=AF.Sigmoid),
                     reads=[B_ps[7]], writes=[B_gate[zi]])
                S.op("dve", lambda e, tp=tp, zi=zi: e.tensor_tensor(out=zf[:, tp, :], in0=zf[:, tp, :], in1=gate[zi], op=ALU.mult),
                     reads=[B_zf, B_gate[zi]], writes=[B_zf])
                S.op("act", lambda e, tp=tp: e.activation(out=junk2, in_=zf[:, tp, :], func=AF.Square, accum_out=st2[:, 0:1]),
                     reads=[B_zf], writes=[B_junk2, B_st2])
                rstd_op(B_st2, st2[:, 0:1], st2[:, 1:2], 1.0 / 512)
                S.op("dve", lambda e, tp=tp, yo=yo: e.scalar_tensor_tensor(out=ybo[yo][:, tp, :], in0=zf[:, tp, :], scalar=st2[:, 1:2],
                                                                          in1=mgrep, op0=ALU.mult, op1=ALU.mult),
                     reads=[B_zf, B_st2, B_winu], writes=[B_ybo[yo]])
            if getattr(C, "dbg_ccut", 9) <= 2:
                continue
            tb0 = seg * seglen + jb * 1024
            S.dma("sp", lambda e, yo=yo, tb0=tb0: e.dma_start(
                out=ybd[tb0:tb0 + 1024, :].rearrange("(j t) c -> j t c", t=8), in_=ybo[yo]), reads=[B_ybo[yo]])

    def ett(out, a_, b_, op):
        S.op("dve", lambda e: e.tensor_tensor(out=out, in0=a_, in1=b_, op=op), **EX)

    def compute_carry():
        S.dma("sp", lambda e: e.dma_start(out=G8, in_=d["exs_dst"].rearrange("(r p) c -> p r c", p=128)),
              reads=[B_exdst], writes=[B_ex])
        S.dma("sp", lambda e: e.dma_start(out=ohs, in_=d["oh"]), writes=[B_ex])
        S.op("dve", lambda e: e.tensor_copy(out=e1, in_=magA), **EX)
        for _ in range(int(np.log2(NJ))):
            ett(e1, e1, e1, ALU.mult)
        S.op("dve", lambda e: e.tensor_copy(out=a256[:, 0, :], in_=dpw_r[:, 7, :]), **EX)
        S.op("dve", lambda e: e.tensor_copy(out=a256[:, 1, :], in_=dpw_i[:, 7, :]), **EX)
        for _ in range(int(np.log2(nwin))):
            ett(e2, a256[:, 0, :], a256[:, 0, :], ALU.mult)
            ett(e3, a256[:, 1, :], a256[:, 1, :], ALU.mult)
            ett(e2, e2, e3, ALU.subtract)
            ett(e3, a256[:, 0, :], a256[:, 1, :], ALU.mult)
            S.op("dve", lambda e: e.tensor_scalar(out=a256[:, 1, :], in0=e3, scalar1=2.0, scalar2=None, op0=ALU.mult), **EX)
            S.op("dve", lambda e: e.tensor_copy(out=a256[:, 0, :], in_=e2), **EX)
        ett(a256[:, 0, :], a256[:, 0, :], e1, ALU.mult)
        ett(a256[:, 1, :], a256[:, 1, :], e1, ALU.mult)
        G8v = G8.rearrange("p r (d i a) -> p r d i a", d=2, i=2)
        S.op("dve", lambda e: e.memset(CH, 0.0), **EX)
        for dr in range(2):
            ls = slice(dr * 16, dr * 16 + 16)
            ar_, ai_ = a256[:, 0, ls], a256[:, 1, ls]
            t1_, t2_ = e2[:, 0:16], e3[:, 0:16]
            order = range(0, NCORES - 1) if dr == 0 else range(NCORES - 1, 0, -1)
            for r in order:
                rn = r + 1 if dr == 0 else r - 1
                cr_, ci_ = CH[:, r, 0, ls], CH[:, r, 1, ls]
                ett(t1_, ar_, cr_, ALU.mult)
                ett(t2_, ai_, ci_, ALU.mult)
                ett(t1_, t1_, t2_, ALU.subtract)
                ett(CH[:, rn, 0, ls], t1_, G8v[:, r, dr, 0, :], ALU.add)
                ett(t1_, ar_, ci_, ALU.mult)
                ett(t2_, ai_, cr_, ALU.mult)
                ett(t1_, t1_, t2_, ALU.add)
                ett(CH[:, rn, 1, ls], t1_, G8v[:, r, dr, 1, :], ALU.add)
        S.op("dve", lambda e: e.memset(Cm, 0.0), **EX)
        for r in range(NCORES):
            S.op("dve", lambda e, r=r: e.scalar_tensor_tensor(out=Cm, in0=CH[:, r, :, :], scalar=ohs[:, r:r + 1], in1=Cm,
                                                              op0=ALU.mult, op1=ALU.add), **EX)
        ett(e1, dpw_r[:, 0, :], Cm[:, 0, :], ALU.mult)
        ett(e2, dpw_i[:, 0, :], Cm[:, 1, :], ALU.mult)
        ett(Yin[:, 0, :], e1, e2, ALU.subtract)
        ett(e1, dpw_r[:, 0, :], Cm[:, 1, :], ALU.mult)
        ett(e2, dpw_i[:, 0, :], Cm[:, 0, :], ALU.mult)
        ett(Yin[:, 1, :], e1, e2, ALU.add)
        for dr in range(2):
            for ri in range(2):
                col = 0 if dr == 0 else NJ
                S.op("dve", lambda e, dr=dr, ri=ri, col=col: e.tensor_copy(
                    out=Xp[dr][ri][:, :, col], in_=Cm[:, ri, dr * 16:dr * 16 + 16]),
                    reads=[B_ex], writes=[B_Xp[dr][ri]])

    sample = nseg - 1 if getattr(C, "exchange", False) else None
    if sample is not None:
        stageA(sample)
        S.barrier()
        stageB(sample, False, True)
        S.dma("sp", lambda e: e.dma_start(out=d["exs_src"], in_=Eloc.rearrange("p d i a -> p (d i a)")),
              reads=[B_ex], writes=[B_exsrc])
        S.coll(lambda e: e.collective_compute("AllGather", ALU.bypass, replica_groups=[list(range(NCORES))],
                                              ins=[d["exs_src"]], outs=[d["exs_dst"]]),
               reads=[B_exsrc], writes=[B_exdst])
        S.barrier_local()
    for seg in range(nseg):
        stageA(seg)
        S.barrier()
        if getattr(C, "dbg_scut", 9) <= 1:
            continue
        if seg == sample:
            compute_carry()
        stageB(seg, seg == sample, False)
        S.barrier()
        if getattr(C, "dbg_scut", 9) <= 2:
            continue
        stageC(seg)
        S.barrier()


_NC_CACHE = {}
NTOK_CORE = NSEG * SEG


def _get_nc():
    if "nc" not in _NC_CACHE:
        _NC_CACHE["nc"] = build(NTOK_CORE, phases="SHF")
    return _NC_CACHE["nc"]


def kernel(x_prompt, x_sample, norm1_g, w_in, hgrn_lb, hgrn_onorm_g, s5_lambda_re, s5_lambda_im,
           s5_log_dt, s5_b_re, s5_b_im, s5_c_re, s5_c_im, s5_d, s5_glu_w, s5_glu_b, s5_merge_g,
           w_out, norm2_g, w_ff1, w_ff2, norm_f_g):
    f = lambda a: np.ascontiguousarray(np.asarray(a, dtype=np.float32))
    shared = {
        "norm1_g": f(norm1_g)[0], "w_in": f(w_in)[0], "hgrn_lb": f(hgrn_lb), "hgrn_onorm_g": f(hgrn_onorm_g)[0],
        "s5_lambda_re": f(s5_lambda_re)[0], "s5_lambda_im": f(s5_lambda_im)[0], "s5_log_dt": f(s5_log_dt)[0],
        "s5_b_re": f(s5_b_re)[0], "s5_b_im": f(s5_b_im)[0], "s5_c_re": f(s5_c_re)[0], "s5_c_im": f(s5_c_im)[0],
        "s5_d": f(s5_d)[0], "s5_glu_w": f(s5_glu_w)[0], "s5_glu_b": f(s5_glu_b)[0], "s5_merge_g": f(s5_merge_g)[0],
        "w_out": f(w_out)[0], "norm2_g": f(norm2_g)[0], "w_ff1": f(w_ff1)[0], "w_ff2": f(w_ff2)[0],
        "norm_f_g": f(norm_f_g),
    }
    shared.update(host_consts())
    xp = f(x_prompt)
    xs = f(x_sample)
    nps = xp.shape[0] // NCORES
    in_maps = []
    for c in range(NCORES):
        xc = np.concatenate([xp[c * nps:(c + 1) * nps].reshape(nps * SEG, D), xs[0, c * SEG:(c + 1) * SEG]], axis=0)
        m = dict(shared)
        m["x"] = np.ascontiguousarray(xc)
        oh = np.zeros((128, NCORES), np.float32)
        oh[:, c] = 1.0
        m["oh"] = oh
        in_maps.append(m)
    nc = _get_nc()
    res = run_bass_kernel_spmd(nc, in_maps, core_ids=list(range(NCORES)))
    y_prompt = np.empty_like(xp)
    y_sample = np.empty_like(xs)
    for c in range(NCORES):
        o = np.asarray(res.results[c]["out"], dtype=np.float32)
        y_prompt[c * nps:(c + 1) * nps] = o[0:nps * SEG].reshape(nps, SEG, D)
        y_sample[0, c * SEG:(c + 1) * SEG] = o[nps * SEG:]
    return (y_prompt, y_sample)
```
